# Optimizing a Trainium2 kernel written in Bass

```python
import math
import jax
import jax.numpy as jnp
from jax import lax
import numpy as np

D_MODEL = 4096
BATCH = 1
SEQ = 8192
DEPTH = 1

D_MIX = D_MODEL
D_RWKV = D_MIX // 2
D_DIFF = D_MIX - D_RWKV
RWKV_HEAD = 64
N_RWKV_HEADS = D_RWKV // RWKV_HEAD
DECAY_RANK = 96
ICLR_RANK = 96
GATE_RANK = 256
DIFF_D = 64
DIFF_HEAD = 2 * DIFF_D
N_DIFF_HEADS = D_DIFF // DIFF_HEAD
D_FF = 256 * ((8 * D_MODEL // 3 + 255) // 256)
Q_BLOCK = 128
ALIBI_MAX_EXP = 8.0
NORM_EPS = 1e-6
RWKV_GN_EPS = 64e-5
SUBLN_EPS = 1e-5
FFN_RESIDUAL = 0.5

RWKV_SPLITS = (D_RWKV, 2 * D_RWKV, 3 * D_RWKV, 3 * D_RWKV + DECAY_RANK, 3 * D_RWKV + DECAY_RANK + ICLR_RANK)
RWKV_COLS = 3 * D_RWKV + DECAY_RANK + ICLR_RANK + GATE_RANK
DIFF_COLS = 3 * D_DIFF
D_IN = RWKV_COLS + DIFF_COLS

kernel_name = "hybrid_rwkv7_diffattn_macaron_block"


def rms_norm(x, g, eps=NORM_EPS):
    xf = x.astype(jnp.float32)
    y = xf * lax.rsqrt(jnp.mean(xf * xf, axis=-1, keepdims=True) + eps)
    return (y * g.astype(jnp.float32)).astype(x.dtype)


def swiglu(x, w_gate, w_up, w_down):
    return (jax.nn.silu(x @ w_gate) * (x @ w_up)) @ w_down


def centred_token_shift(p, mu_prev, mu_next):
    zero = jnp.zeros_like(p[:, :1])
    p_prev = jnp.concatenate([zero, p[:, :-1]], axis=1)
    p_next = jnp.concatenate([p[:, 1:], zero], axis=1)
    return p + mu_prev * (p_prev - p) + mu_next * (p_next - p)


def wkv7_scan(r, decay, k, v, kk, b, reverse):
    bsz, _, h, n = r.shape

    def step(S, inp):
        r_t, w_t, k_t, v_t, kk_t, b_t = inp
        sa = jnp.einsum('bhvk,bhk->bhv', S, -kk_t)
        S = S * w_t[:, :, None, :] + sa[..., :, None] * b_t[..., None, :] + v_t[..., :, None] * k_t[..., None, :]
        y = jnp.einsum('bhvk,bhk->bhv', S, r_t)
        return S, y

    xs = tuple(jnp.swapaxes(a.astype(jnp.float32), 0, 1) for a in (r, decay, k, v, kk, b))
    S0 = jnp.zeros((bsz, h, n, n), jnp.float32)
    _, ys = lax.scan(step, S0, xs, reverse=reverse)
    return jnp.swapaxes(ys, 0, 1)


def rwkv7_bidir(p, mu_prev, mu_next, w0_f, w2_f, w0_b, w2_b, a0_f, a2_f, a0_b, a2_b,
                g2, k_k, k_a, r_k, gn_w, gn_b):
    bsz, t, _ = p.shape
    H, N = N_RWKV_HEADS, RWKV_HEAD
    heads = lambda a: a.reshape(bsz, t, H, N)
    p = centred_token_shift(p, mu_prev, mu_next)
    r, k, v, xw, xa, xg = jnp.split(p, list(RWKV_SPLITS), axis=-1)
    hw = jnp.tanh(xw)
    g = jax.nn.sigmoid(xg) @ g2
    kk = heads((k * k_k).astype(jnp.float32))
    kk = kk / jnp.maximum(jnp.linalg.norm(kk, axis=-1, keepdims=True), 1e-12)
    r_h = heads(r).astype(jnp.float32)
    v_h = heads(v).astype(jnp.float32)
    k_h = heads(k).astype(jnp.float32)
    k_a_h = k_a.astype(jnp.float32).reshape(H, N)
    y = jnp.zeros((bsz, t, H, N), jnp.float32)
    k_sum = jnp.zeros((bsz, t, H, N), jnp.float32)
    for w0, w2, a0, a2, rev in ((w0_f, w2_f, a0_f, a2_f, False), (w0_b, w2_b, a0_b, a2_b, True)):
        w = -jax.nn.softplus(-(w0 + hw @ w2)) - 0.5
        decay = jnp.exp(-jnp.exp(w.astype(jnp.float32)))
        a = heads(jax.nn.sigmoid(a0 + xa @ a2).astype(jnp.float32))
        k_dir = k_h * (1.0 + (a - 1.0) * k_a_h)
        y = y + wkv7_scan(r_h, heads(decay), k_dir, v_h, kk, kk * a, rev)
        k_sum = k_sum + k_dir
    mean = jnp.mean(y, axis=-1, keepdims=True)
    var = jnp.mean(jnp.square(y - mean), axis=-1, keepdims=True)
    yn = ((y - mean) * lax.rsqrt(var + RWKV_GN_EPS)).reshape(bsz, t, D_RWKV) * gn_w.astype(jnp.float32) + gn_b.astype(jnp.float32)
    bonus = jnp.sum(r_h * k_sum * r_k.astype(jnp.float32), axis=-1, keepdims=True) * v_h
    out = (yn + bonus.reshape(bsz, t, D_RWKV)) * g.astype(jnp.float32)
    return out.astype(p.dtype)


def diff_attention_alibi(p, lq1, lk1, lq2, lk2, subln_g, lam_init):
    bsz, t, _ = p.shape
    H, d = N_DIFF_HEADS, DIFF_D
    q, k, v = jnp.split(p, 3, axis=-1)
    q = q.reshape(bsz, t, H, 2, d)
    k = k.reshape(bsz, t, H, 2, d)
    v = v.reshape(bsz, t, H, 2 * d)
    lam = (jnp.exp(jnp.sum(lq1.astype(jnp.float32) * lk1.astype(jnp.float32)))
           - jnp.exp(jnp.sum(lq2.astype(jnp.float32) * lk2.astype(jnp.float32))) + lam_init)
    slopes = 2.0 ** (-ALIBI_MAX_EXP * jnp.arange(1, H + 1, dtype=jnp.float32) / H)
    kpos = jnp.arange(t, dtype=jnp.float32)
    scale = d ** -0.5
    nblk = t // Q_BLOCK
    qb = jnp.transpose(q.reshape(bsz, nblk, Q_BLOCK, H, 2, d), (1, 0, 2, 3, 4, 5))

    def block(args):
        q_blk, i = args
        qpos = (i * Q_BLOCK + jnp.arange(Q_BLOCK)).astype(jnp.float32)
        s = jnp.einsum('bqhcd,bkhcd->bchqk', q_blk, k).astype(jnp.float32) * scale
        dist = jnp.abs(qpos[:, None] - kpos[None, :])
        s = s - slopes[:, None, None] * dist[None]
        prob = jax.nn.softmax(s, axis=-1)
        attn = prob[:, 0] - lam * prob[:, 1]
        return jnp.einsum('bhqk,bkhv->bqhv', attn.astype(v.dtype), v)

    ob = lax.map(block, (qb, jnp.arange(nblk)))
    o = jnp.transpose(ob, (1, 0, 2, 3, 4)).reshape(bsz, t, H, 2 * d)
    o = rms_norm(o, subln_g, SUBLN_EPS) * (1.0 - lam_init)
    return o.reshape(bsz, t, D_DIFF)


def setup_inputs(seed: int = 0) -> dict:
    key = jax.random.key(seed)
    ks = iter(jax.random.split(key, 64))
    L, D, F = DEPTH, D_MODEL, D_FF
    nrm = lambda shape, s: s * jax.random.normal(next(ks), shape, jnp.float32)
    gain = lambda n: 1.0 + nrm((L, n), 0.02)
    unif = lambda shape, lo, hi: jax.random.uniform(next(ks), shape, jnp.float32, lo, hi)
    return {
        "x": nrm((BATCH, SEQ, D), 1.0),
        "ffn1_pre_g": gain(D),
        "ffn1_w_gate": nrm((L, D, F), D ** -0.5),
        "ffn1_w_up": nrm((L, D, F), D ** -0.5),
        "ffn1_w_down": nrm((L, F, D), F ** -0.5),
        "ffn1_post_g": gain(D),
        "mix_pre_g": gain(D),
        "w_in": nrm((L, D, D_IN), D ** -0.5),
        "mu_prev": unif((L, RWKV_COLS), 0.0, 0.5),
        "mu_next": unif((L, RWKV_COLS), 0.0, 0.5),
        "w0_f": unif((L, D_RWKV), -7.0, -1.0),
        "w2_f": nrm((L, DECAY_RANK, D_RWKV), 0.1 * DECAY_RANK ** -0.5),
        "w0_b": unif((L, D_RWKV), -7.0, -1.0),
        "w2_b": nrm((L, DECAY_RANK, D_RWKV), 0.1 * DECAY_RANK ** -0.5),
        "a0_f": nrm((L, D_RWKV), 0.1),
        "a2_f": nrm((L, ICLR_RANK, D_RWKV), ICLR_RANK ** -0.5),
        "a0_b": nrm((L, D_RWKV), 0.1),
        "a2_b": nrm((L, ICLR_RANK, D_RWKV), ICLR_RANK ** -0.5),
        "g2": nrm((L, GATE_RANK, D_RWKV), GATE_RANK ** -0.5),
        "k_k": 0.85 + nrm((L, D_RWKV), 0.02),
        "k_a": 1.0 + nrm((L, D_RWKV), 0.02),
        "r_k": nrm((L, N_RWKV_HEADS, RWKV_HEAD), 0.1),
        "gn_w": gain(D_RWKV),
        "gn_b": nrm((L, D_RWKV), 0.02),
        "lq1": nrm((L, DIFF_D), 0.1),
        "lk1": nrm((L, DIFF_D), 0.1),
        "lq2": nrm((L, DIFF_D), 0.1),
        "lk2": nrm((L, DIFF_D), 0.1),
        "subln_g": gain(DIFF_HEAD),
        "w_out": nrm((L, D_MIX, D), D_MIX ** -0.5),
        "mix_post_g": gain(D),
        "ffn2_pre_g": gain(D),
        "ffn2_w_gate": nrm((L, D, F), D ** -0.5),
        "ffn2_w_up": nrm((L, D, F), D ** -0.5),
        "ffn2_w_down": nrm((L, F, D), F ** -0.5),
        "ffn2_post_g": gain(D),
        "final_g": gain(D),
    }


def reference(x, ffn1_pre_g, ffn1_w_gate, ffn1_w_up, ffn1_w_down, ffn1_post_g,
              mix_pre_g, w_in, mu_prev, mu_next, w0_f, w2_f, w0_b, w2_b,
              a0_f, a2_f, a0_b, a2_b, g2, k_k, k_a, r_k, gn_w, gn_b,
              lq1, lk1, lq2, lk2, subln_g, w_out, mix_post_g,
              ffn2_pre_g, ffn2_w_gate, ffn2_w_up, ffn2_w_down, ffn2_post_g, final_g):
    h = x
    for l in range(DEPTH):
        lam_init = 0.8 - 0.6 * math.exp(-0.3 * l)
        f1 = swiglu(rms_norm(h, ffn1_pre_g[l]), ffn1_w_gate[l], ffn1_w_up[l], ffn1_w_down[l])
        h = h + FFN_RESIDUAL * rms_norm(f1, ffn1_post_g[l])
        proj = rms_norm(h, mix_pre_g[l]) @ w_in[l]
        y_a = rwkv7_bidir(proj[..., :RWKV_COLS], mu_prev[l], mu_next[l], w0_f[l], w2_f[l], w0_b[l], w2_b[l],
                          a0_f[l], a2_f[l], a0_b[l], a2_b[l], g2[l], k_k[l], k_a[l], r_k[l], gn_w[l], gn_b[l])
        y_b = diff_attention_alibi(proj[..., RWKV_COLS:], lq1[l], lk1[l], lq2[l], lk2[l], subln_g[l], lam_init)
        mix = jnp.concatenate([y_a, y_b.astype(y_a.dtype)], axis=-1) @ w_out[l]
        h = h + rms_norm(mix, mix_post_g[l])
        f2 = swiglu(rms_norm(h, ffn2_pre_g[l]), ffn2_w_gate[l], ffn2_w_up[l], ffn2_w_down[l])
        h = h + FFN_RESIDUAL * rms_norm(f2, ffn2_post_g[l])
        h = rms_norm(h, final_g[l])
    return h
```

```python
import numpy as np
from contextlib import ExitStack
import concourse.bass as bass
import concourse.mybir as mybir
from concourse.bass_utils import run_bass_kernel_spmd

F32 = mybir.dt.float32
BF16 = mybir.dt.bfloat16
AF = mybir.ActivationFunctionType
ALU = mybir.AluOpType
AX = mybir.AxisListType

NCORES = 8
D = 4096
SEQ = 8192
NT = SEQ // NCORES
DFF = 11008
KC = D // 128
FC = DFF // 128
TT = NT // 128
NORM_EPS = 1e-6


CLEAR_SEMS = False


class Sched:
    def __init__(self, nc, stack, n_dma_sems=56):
        self.nc = nc
        self.engs = {"pe": nc.tensor, "act": nc.scalar, "dve": nc.vector,
                     "pool": nc.gpsimd, "sp": nc.sync}
        self.ops = []
        self.dma_sems = [stack.enter_context(nc.semaphore("dq%d" % k)) for k in range(n_dma_sems)]
        self.eng_sems = {e: stack.enter_context(nc.semaphore("es_" + e)) for e in self.engs}
        self.sem_running = [0] * n_dma_sems
        self.cnt = {e: 0 for e in self.engs}
        self.total = 0

    ktag = None
    KTAG_NAMES = frozenset(("sigw", "aic", "kd", "bd", "gsb", "kk", "kk2", "rn", "kkn", "tmpa", "ksum", "rk", "bon",
                            "sigT", "xTt", "gi", "ge", "gv", "gs", "ahat", "rhat", "bhat", "khat", "btil", "ktil"))

    def _tag(self, k):
        base = k[0] if isinstance(k, tuple) else k
        if base in self.KTAG_NAMES:
            return ("c%d" % self.ktag, k)
        return k

    def add(self, eng, fn, reads=(), writes=(), dma=None):
        if self.ktag is not None:
            reads = [self._tag(k) for k in reads]
            writes = [self._tag(k) for k in writes]
        self.ops.append((eng, fn, tuple(reads), tuple(writes), dma))

    def emit(self):
        nc = self.nc
        ops = self.ops
        n = len(ops)
        nsem = len(self.dma_sems)
        last_w = {}
        rd_eng = {}
        rd_dma = {}
        deps = [None] * n
        raw = [None] * n
        for i, (eng, fn, R, W, dma) in enumerate(ops):
            d = set()
            rw = set()
            for r in R:
                j = last_w.get(r)
                if j is not None:
                    d.add(j)
                    rw.add(j)
            raw[i] = rw
            for w in W:
                j = last_w.get(w)
                if j is not None:
                    d.add(j)
                for j in rd_eng.get(w, {}).values():
                    d.add(j)
                for j in rd_dma.get(w, ()):
                    d.add(j)
            d.discard(i)
            deps[i] = d
            for r in R:
                if dma is not None:
                    rd_dma.setdefault(r, []).append(i)
                else:
                    rd_eng.setdefault(r, {})[eng] = i
            for w in W:
                last_w[w] = i
                rd_eng[w] = {}
                rd_dma[w] = []
        key_sem = {}
        dma_val = [0] * n
        for i, (eng, fn, R, W, dma) in enumerate(ops):
            if dma is not None:
                if dma not in key_sem:
                    key_sem[dma] = len(key_sem) % nsem
                s = key_sem[dma]
                self.sem_running[s] += 16
                dma_val[i] = self.sem_running[s]
        waited = {e: {f: -1 for f in self.engs} for e in self.engs}
        waited_dma = {e: {} for e in self.engs}
        waits = [None] * n
        signal = [False] * n
        for i, (eng, fn, R, W, dma) in enumerate(ops):
            wl = []
            cw = {}
            dw = {}
            for j in deps[i]:
                ej, _, _, _, dj = ops[j]
                if dj is not None:
                    s = key_sem[dj]
                    v = dma_val[j]
                    if v > waited_dma[eng].get(s, 0):
                        dw[s] = max(dw.get(s, 0), v)
                else:
                    if ej == eng and eng == "pe":
                        continue
                    if j > waited[eng][ej]:
                        cw[ej] = max(cw.get(ej, -1), j)
            for ej, j in cw.items():
                waited[eng][ej] = j
                signal[j] = True
                wl.append(("c", ej, j))
            for s, v in dw.items():
                waited_dma[eng][s] = v
                wl.append(("d", s, v))
            waits[i] = wl
        tok = [0] * n
        for i, (eng, fn, R, W, dma) in enumerate(ops):
            if dma is None and signal[i]:
                self.cnt[eng] += 1
                tok[i] = self.cnt[eng]
        for i, (eng, fn, R, W, dma) in enumerate(ops):
            e = self.engs[eng]
            for w in waits[i]:
                if w[0] == "c":
                    e.wait_ge(self.eng_sems[w[1]], tok[w[2]])
                else:
                    e.wait_ge(self.dma_sems[w[1]], w[2])
            ins = fn(e)
            if dma is not None:
                ins.then_inc(self.dma_sems[key_sem[dma]], 16)
            elif signal[i]:
                ins.then_inc(self.eng_sems[eng], 1)
        for s in range(nsem):
            if self.sem_running[s] > 0:
                nc.sync.wait_ge(self.dma_sems[s], self.sem_running[s])
        nc.all_engine_barrier()
        if not CLEAR_SEMS:
            self.ops = []
            self.total += n
            return n
        for s in range(nsem):
            if self.sem_running[s] > 0:
                nc.sync.sem_clear(self.dma_sems[s])
                self.sem_running[s] = 0
        for e in self.engs:
            if self.cnt[e] > 0:
                nc.sync.sem_clear(self.eng_sems[e])
                self.cnt[e] = 0
        nc.all_engine_barrier()
        self.ops = []
        self.total += n
        return n


class Builder:
    def __init__(self):
        self.nc = bass.Bass("TRN2", target_bir_lowering=False)
        self.stack = ExitStack()
        self.S = Sched(self.nc, self.stack)
        self.rr = {}

    def rot(self, name, n):
        k = self.rr.get(name, 0)
        self.rr[name] = k + 1
        return k % n

    def sb(self, name, shape, dt, stack=None):
        return (stack or self.stack).enter_context(self.nc.sbuf_tensor(name, list(shape), dt))

    def dram(self, name, shape, dt, kind="Internal"):
        return self.nc.dram_tensor(name, list(shape), dt, kind=kind).ap()

    def dma(self, q, out, in_, key, reads=(), writes=(), **kw):
        self.S.add(q, lambda e, o=out, i=in_, kw=kw: e.dma_start(out=o, in_=i, **kw),
                   reads=reads, writes=writes, dma=key)

    def evac_copy(self, out, in_, reads, writes):
        if self.rot("evac", 2) == 0:
            self.S.add("act", lambda e, o=out, i=in_: e.copy(out=o, in_=i), reads, writes)
        else:
            self.S.add("dve", lambda e, o=out, i=in_: e.tensor_copy(out=o, in_=i), reads, writes)

    def setup_common(self):
        nc = self.nc
        self.ps_all = self.stack.enter_context(nc.psum_tensor("ps_all", [128, 4096], F32))
        self.ps = [self.ps_all[:, i * 512:(i + 1) * 512] for i in range(8)]
        self.ident_f = self.sb("ident_f", [128, 128], F32)
        self.ident_b = self.sb("ident_b", [128, 128], BF16)
        S = self.S
        idf, idb = self.ident_f, self.ident_b
        S.add("pool", lambda e: e.memset(idf[:], 0.0), (), ("ident_f",))
        S.add("pool", lambda e: e.affine_select(
            out=idf[:], in_=idf[:], pattern=[[-1, 128]], compare_op=ALU.not_equal,
            fill=1.0, base=0, channel_multiplier=1), ("ident_f",), ("ident_f",))
        S.add("pool", lambda e: e.tensor_copy(out=idb[:], in_=idf[:]), ("ident_f",), ("ident_b",))

    def alloc_token_phase(self, st):
        self.xT = self.sb("xT", [128, KC, NT], BF16, st)
        self.rt = [self.sb("rt%d" % i, [128, D], F32, st) for i in range(2)]
        self.xs = self.sb("xs", [128, D], BF16, st)
        self.gbc = [self.sb("gbc%d" % i, [128, D], F32, st) for i in range(2)]
        self.ss = self.sb("ss", [128, 8], F32, st)
        self.W16 = [self.sb("W16_%d" % i, [128, 1024], BF16, st) for i in range(4)]
        self.X = [self.sb("X%d" % i, [128, 1024], F32, st) for i in range(6)]

    def load_gbc(self, k, vec_ap):
        self.dma("sp", self.gbc[k][:], vec_ap.partition_broadcast(128), ("gbc", k), (), (("gbc", k),))

    def rstd_of(self, tile, tile_key, col):
        S, ss, xs = self.S, self.ss, self.xs
        S.add("act", lambda e: e.activation(out=xs[:], in_=tile[:], func=AF.Square,
                                            accum_out=ss[:, col:col + 1]),
              (tile_key,), ("xs", ("ss", col)))
        S.add("act", lambda e: e.activation(out=ss[:, col:col + 1], in_=ss[:, col:col + 1],
                                            func=AF.Sqrt, bias=NORM_EPS, scale=1.0 / D),
              (("ss", col),), (("ss", col),))
        S.add("dve", lambda e: e.reciprocal(out=ss[:, col:col + 1], in_=ss[:, col:col + 1]),
              (("ss", col),), (("ss", col),))

    def rows_to_xT(self, tile, tile_key, gk, tt, col):
        S, xs, ss, xT, idb = self.S, self.xs, self.ss, self.xT, self.ident_b
        gb = self.gbc[gk]
        S.add("dve", lambda e: e.scalar_tensor_tensor(out=xs[:], in0=tile[:], scalar=ss[:, col:col + 1],
                                                      in1=gb[:], op0=ALU.mult, op1=ALU.mult),
              (tile_key, ("ss", col), ("gbc", gk)), ("xs",))
        for q in range(KC // 4):
            b = self.rot("tpb", 2)
            psb = self.ps[b][:].bitcast(BF16)
            for j in range(4):
                dc = q * 4 + j
                S.add("pe", lambda e, j=j, dc=dc, psb=psb: e.transpose(
                    out=psb[:, j * 128:(j + 1) * 128], in_=xs[:, dc * 128:(dc + 1) * 128],
                    identity=idb[:]), ("xs", "ident_b"), (("ps", b),))
            dst = xT[:, q * 4:(q + 1) * 4, tt * 128:(tt + 1) * 128]
            src = psb[:, 0:512].rearrange("p (j t) -> p j t", j=4)
            self.evac_copy(dst, src, (("ps", b),), (("xT", tt, q),))

    def norm_phase(self, src, g_vec, src_key="x"):
        self.load_gbc(0, g_vec)
        for tt in range(TT):
            k = self.rot("rt", 2)
            self.dma("sp", self.rt[k][:], src[tt * 128:(tt + 1) * 128, :], ("rt", k),
                     ((src_key, tt),), (("rt", k),))
            self.rstd_of(self.rt[k], ("rt", k), tt)
            self.rows_to_xT(self.rt[k], ("rt", k), 0, tt, tt)

    def xT_keys(self, dc, half):
        return [("xT", tt, dc // 4) for tt in range(half * 4, half * 4 + 4)]

    def ffn_up(self, wg, wu, hT_dram):
        S = self.S
        for fg in range(FC // 2):
            sgk = self.rot("sg", 2)
            sg = [self.X[sgk * 2], self.X[sgk * 2 + 1]]
            sgkeys = [("X", sgk * 2), ("X", sgk * 2 + 1)]
            hk = 4 + self.rot("hst", 2)
            hst = self.X[hk][:].bitcast(BF16).rearrange("p (c t) -> p c t", c=2)
            for mi, W in enumerate((wg, wu)):
                for dq in range(KC // 4):
                    k = self.rot("W16", 4)
                    wbf = self.W16[k][:].rearrange("p (c f) -> p c f", c=4)
                    src = W[dq * 512:(dq + 1) * 512, fg * 256:(fg + 1) * 256].rearrange(
                        "(c p) f -> p c f", p=128)
                    self.dma("pool", wbf, src, ("W16", k), (), (("W16", k),))
                    for c in range(4):
                        dc = dq * 4 + c
                        for fc in range(2):
                            for half in range(2):
                                b = mi * 4 + fc * 2 + half
                                S.add("pe", lambda e, b=b, c=c, fc=fc, half=half, dc=dc, wbf=wbf: e.matmul(
                                    self.ps[b][:], lhsT=wbf[:, c, fc * 128:(fc + 1) * 128],
                                    rhs=self.xT[:, dc, half * 512:(half + 1) * 512],
                                    start=(dc == 0), stop=(dc == KC - 1)),
                                    [("W16", k)] + self.xT_keys(dc, half), (("ps", b),))
                if mi == 0:
                    for fc in range(2):
                        for half in range(2):
                            b = fc * 2 + half
                            S.add("act", lambda e, b=b, fc=fc, half=half, sg=sg: e.activation(
                                out=sg[fc][:, half * 512:(half + 1) * 512], in_=self.ps[b][:], func=AF.Silu),
                                (("ps", b),), (sgkeys[fc],))
                else:
                    for fc in range(2):
                        for half in range(2):
                            b = 4 + fc * 2 + half
                            S.add("dve", lambda e, b=b, fc=fc, half=half, sg=sg, hst=hst: e.tensor_tensor(
                                out=hst[:, fc, half * 512:(half + 1) * 512], in0=self.ps[b][:],
                                in1=sg[fc][:, half * 512:(half + 1) * 512], op=ALU.mult),
                                (("ps", b), sgkeys[fc]), (("X", hk),))
            self.dma("act", hT_dram[fg * 2:(fg + 1) * 2].rearrange("c p t -> p c t"), hst,
                     ("hst", hk), (("X", hk),), (("hT", fg),))

    def ffn_down(self, hT_dram, wd, f1_dram, f_key="f1"):
        S = self.S
        NQ = FC // 2
        for dt in range(D // 512):
            for fq in range(NQ):
                k = self.rot("W16", 4)
                wbf = self.W16[k][:].rearrange("p (c n) -> p c n", c=2)
                src = wd[fq * 256:(fq + 1) * 256, dt * 512:(dt + 1) * 512].rearrange("(c p) n -> p c n", p=128)
                self.dma("pool", wbf, src, ("W16", k), (), (("W16", k),))
                hk = self.rot("hin", 4)
                hin = self.X[hk][:].bitcast(BF16).rearrange("p (c t) -> p c t", c=2)
                self.dma("sp", hin, hT_dram[fq * 2:(fq + 1) * 2].rearrange("c p t -> p c t"),
                         ("hin", hk), (("hT", fq),), (("X", hk),))
                for c in range(2):
                    for tt in range(TT):
                        S.add("pe", lambda e, c=c, tt=tt, hin=hin, wbf=wbf, fq=fq: e.matmul(
                            self.ps[tt][:], lhsT=hin[:, c, tt * 128:(tt + 1) * 128], rhs=wbf[:, c, :],
                            start=(fq == 0 and c == 0), stop=(fq == NQ - 1 and c == 1)),
                            (("X", hk), ("W16", k)), (("ps", tt),))
            for tt in range(TT):
                ok = 4 + self.rot("ost", 2)
                ost = self.X[ok][:, 0:512]
                self.evac_copy(ost, self.ps[tt][:], (("ps", tt),), (("X", ok),))
                self.dma("act", f1_dram[tt * 128:(tt + 1) * 128, dt * 512:(dt + 1) * 512], ost,
                         ("ost", ok), (("X", ok),), ((f_key, tt, dt),))

    def post_phase(self, f_dram, h_src, g_post, scale, h_dst, next_g, final_out=None,
                   f_key="f1", hs_key="x", hd_key="h"):
        S = self.S
        self.load_gbc(0, g_post)
        self.load_gbc(1, next_g)
        r0, r1 = self.rt
        for tt in range(TT):
            rows = slice(tt * 128, (tt + 1) * 128)
            self.dma("sp", r0[:], f_dram[rows, :], ("rt", 0),
                     [(f_key, tt, dt) for dt in range(D // 512)], (("rt", 0),))
            self.dma("sp", r1[:], h_src[rows, :], ("rt", 1), ((hs_key, tt),), (("rt", 1),))
            self.rstd_of(r0, ("rt", 0), 0)
            S.add("dve", lambda e: e.scalar_tensor_tensor(out=r0[:], in0=r0[:], scalar=self.ss[:, 0:1],
                                                          in1=self.gbc[0][:], op0=ALU.mult, op1=ALU.mult),
                  (("rt", 0), ("ss", 0), ("gbc", 0)), (("rt", 0),))
            S.add("dve", lambda e: e.scalar_tensor_tensor(out=r1[:], in0=r0[:], scalar=float(scale),
                                                          in1=r1[:], op0=ALU.mult, op1=ALU.add),
                  (("rt", 0), ("rt", 1)), (("rt", 1),))
            if h_dst is not None:
                self.dma("act", h_dst[rows, :], r1[:], ("hdst",), (("rt", 1),), ((hd_key, tt),))
            self.rstd_of(r1, ("rt", 1), 1)
            if final_out is None:
                self.rows_to_xT(r1, ("rt", 1), 1, tt, 1)
            else:
                S.add("dve", lambda e: e.scalar_tensor_tensor(out=r0[:], in0=r1[:], scalar=self.ss[:, 1:2],
                                                              in1=self.gbc[1][:], op0=ALU.mult, op1=ALU.mult),
                      (("rt", 1), ("ss", 1), ("gbc", 1)), (("rt", 0),))
                self.dma("act", final_out[rows, :], r0[:], ("fout",), (("rt", 0),), ())

    def linear_tok(self, W, col_tiles, sink):
        S = self.S
        for ci, (c0, wd_) in enumerate(col_tiles):
            for dq in range(KC // 2):
                k = self.rot("W16", 4)
                wbf = self.W16[k][:].rearrange("p (c n) -> p c n", c=2)
                src = W[dq * 256:(dq + 1) * 256, c0:c0 + wd_].rearrange("(c p) n -> p c n", p=128)
                self.dma("pool", wbf[:, :, 0:wd_], src, ("W16", k), (), (("W16", k),))
                for c in range(2):
                    dc = dq * 2 + c
                    for tt in range(TT):
                        S.add("pe", lambda e, c=c, tt=tt, dc=dc, wbf=wbf, wd_=wd_: e.matmul(
                            self.ps[tt][:, 0:wd_], lhsT=self.xT[:, dc, tt * 128:(tt + 1) * 128],
                            rhs=wbf[:, c, 0:wd_], start=(dc == 0), stop=(dc == KC - 1)),
                            (("W16", k), ("xT", tt, dc // 4)), (("ps", tt),))
            for tt in range(TT):
                ok = 4 + self.rot("ost", 2)
                ost = self.X[ok][:, 0:wd_]
                self.evac_copy(ost, self.ps[tt][:, 0:wd_], (("ps", tt),), (("X", ok),))
                sink(tt, ci, ost, ok)


def build_A():
    B = Builder()
    nc = B.nc
    I = lambda n, s: nc.dram_tensor(n, list(s), F32, kind="ExternalInput").ap()
    x = I("x", [NT, D])
    g_pre, g_post, g_next = I("g_pre", [1, D]), I("g_post", [1, D]), I("g_next", [1, D])
    wg, wu, wd = I("wg", [D, DFF]), I("wu", [D, DFF]), I("wd", [DFF, D])
    h1 = nc.dram_tensor("h1", [NT, D], F32, kind="ExternalOutput").ap()
    xn2T = nc.dram_tensor("xn2T", [128, KC, NT], BF16, kind="ExternalOutput").ap()
    hT = B.dram("hT", [FC, 128, NT], BF16)
    f1 = B.dram("f1", [NT, D], F32)
    B.setup_common()
    B.alloc_token_phase(B.stack)
    B.norm_phase(x, g_pre[0])
    B.ffn_up(wg, wu, hT)
    B.ffn_down(hT, wd, f1)
    B.post_phase(f1, x, g_post[0], 0.5, h1, g_next[0])
    B.dma("sp", xn2T, B.xT[:], "xo", [("xT", tt, q) for tt in range(TT) for q in range(KC // 4)], ())
    B.S.emit()
    return B


def build_C():
    B = Builder()
    nc = B.nc
    I = lambda n, s: nc.dram_tensor(n, list(s), F32, kind="ExternalInput").ap()
    h1 = I("h1", [NT, D])
    mixT = nc.dram_tensor("mixT", [128, KC, NT], BF16, kind="ExternalInput").ap()
    w_out = I("w_out", [D, D])
    g_mpost, g_pre, g_post, g_fin = I("g_mpost", [1, D]), I("g_pre", [1, D]), I("g_post", [1, D]), I("g_fin", [1, D])
    wg, wu, wd = I("wg", [D, DFF]), I("wu", [D, DFF]), I("wd", [DFF, D])
    out = nc.dram_tensor("out", [NT, D], F32, kind="ExternalOutput").ap()
    hT = B.dram("hT", [FC, 128, NT], BF16)
    f2 = B.dram("f2", [NT, D], F32)
    mx = B.dram("mx", [NT, D], F32)
    h2 = B.dram("h2", [NT, D], F32)
    B.setup_common()
    B.alloc_token_phase(B.stack)
    B.dma("sp", B.xT[:], mixT, "xi", (), [("xT", tt, q) for tt in range(TT) for q in range(KC // 4)])

    def sink(tt, ci, ost, ok):
        B.dma("act", mx[tt * 128:(tt + 1) * 128, ci * 512:(ci + 1) * 512], ost, ("ost", ok),
              (("X", ok),), (("mx", tt, ci),))
    B.linear_tok(w_out, [(c * 512, 512) for c in range(D // 512)], sink)
    B.post_phase(mx, h1, g_mpost[0], 1.0, h2, g_pre[0], f_key="mx", hs_key="h1", hd_key="h2")
    B.ffn_up(wg, wu, hT)
    B.ffn_down(hT, wd, f2, f_key="f2")
    B.post_phase(f2, h2, g_post[0], 0.5, None, g_fin[0], final_out=out, f_key="f2", hs_key="h2")
    B.S.emit()
    return B


I32 = mybir.dt.int32
NBLK = SEQ // NT
RC_OFF = [0, 128, 256, 384, 512, 640, 768, 864, 960, 1088, 1216, 1344, 1472, 1600]
RC_ROWS = [128] * 6 + [96, 96] + [128] * 6
DV_OFF = 1728
WB_COLS = 1984
LN2 = 0.6931471805599453
SUBLN_EPS = 1e-5
LAM_INIT = 0.2


class BuilderB(Builder):
    def proj_phase(self, xnT_all, Wb, PR, st):
        S = self.S
        xT = self.sb("xTB", [128, KC, NT], BF16, st)
        wr = [self.sb("wr%d" % i, [128, KC, 128], BF16, st) for i in range(2)]
        stg = [self.sb("stg%d" % i, [128, 512], F32, st) for i in range(3)]

        zt = stg[0]
        S.add("dve", lambda e: e.memset(zt[:, 0:1], 0.0), (), (("stg", 0),))
        for rc in range(10):
            self.dma("sp", PR[rc][:, 0:1], zt[:, 0:1], "prz", (("stg", 0),), (("PRz", rc),),
                     allow_slow_non_contiguous=True)
            self.dma("sp", PR[rc][:, SEQ + 1:SEQ + 2], zt[:, 0:1], "prz", (("stg", 0),), (("PRz", rc),),
                     allow_slow_non_contiguous=True)
        for blk in range(NBLK):
            self.dma("sp", xT[:], xnT_all[blk], "xTB", (), ("xTB",))
            for rc in range(14):
                rows = RC_ROWS[rc]
                k = self.rot("wr", 2)
                src = Wb[:, RC_OFF[rc]:RC_OFF[rc] + rows].rearrange("(dc p) n -> p dc n", p=128)
                self.dma("pool", wr[k][:, :, 0:rows], src, ("wr", k), (), (("wr", k),))
                for half in range(2):
                    b = self.rot("pb1", 4)
                    for dc in range(KC):
                        S.add("pe", lambda e, b=b, dc=dc, k=k, rows=rows, half=half: e.matmul(
                            self.ps[b][0:rows, :], lhsT=wr[k][:, dc, 0:rows],
                            rhs=xT[:, dc, half * 512:(half + 1) * 512],
                            start=(dc == 0), stop=(dc == KC - 1)), (("wr", k), "xTB"), (("ps", b),))
                    t0 = blk * NT + half * 512
                    if rc < 10:
                        sk = self.rot("stg", 3)
                        self.evac_copy(stg[sk][0:rows, :], self.ps[b][0:rows, :], (("ps", b),), (("stg", sk),))
                        self.dma("act", PR[rc][0:rows, 1 + t0:1 + t0 + 512], stg[sk][0:rows, :], ("stgo", sk),
                                 (("stg", sk),), (("PR", rc, blk),))
                    elif rc < 12:
                        h = rc - 10
                        S.add("act", lambda e, b=b, h=h, t0=t0: e.activation(
                            out=self.qT[h][:, t0:t0 + 512], in_=self.ps[b][:], func=AF.Identity,
                            scale=self.qs[:, h:h + 1]), (("ps", b), "qs"), (("qT", h),))
                    else:
                        h = rc - 12
                        self.evac_copy(self.kT[h][:, t0:t0 + 512], self.ps[b][:], (("ps", b),), (("kT", h),))
            for h in range(2):
                k = self.rot("wr", 2)
                src = Wb[:, DV_OFF + h * 128:DV_OFF + (h + 1) * 128].rearrange("(dc p) n -> p dc n", p=128)
                self.dma("pool", wr[k][:], src, ("wr", k), (), (("wr", k),))
                for tt in range(TT):
                    b = self.rot("pb1", 4)
                    for dc in range(KC):
                        S.add("pe", lambda e, b=b, dc=dc, k=k, tt=tt: e.matmul(
                            self.ps[b][:, 0:128], lhsT=xT[:, dc, tt * 128:(tt + 1) * 128],
                            rhs=wr[k][:, dc, :], start=(dc == 0), stop=(dc == KC - 1)),
                            (("wr", k), "xTB"), (("ps", b),))
                    self.evac_copy(self.Vx[h][:, blk * TT + tt, 0:128], self.ps[b][:, 0:128],
                                   (("ps", b),), (("Vx", h),))

    def attn_consts(self, hidx, lq1, lk1, lq2, lk2, subln_g, st):
        S = self.S
        self.qs = self.sb("qs", [128, 2], F32, st)
        self.slope = self.sb("slope", [128, 2], F32, st)
        self.negc = self.sb("negc", [128, 2], F32, st)
        self.lam = self.sb("lam", [128, 4], F32, st)
        self.gsub = self.sb("gsub", [128, 128], F32, st)
        hid = self.sb("hid", [128, 2], F32, st)
        lt = self.sb("lt", [128, 4, 64], F32, st)
        self.dma("sp", hid[:], hidx, "c0", (), ("hid",))
        for i, v in enumerate((lq1, lk1, lq2, lk2)):
            self.dma("sp", lt[:, i, :], v.partition_broadcast(128), "c1", (), ("lt",))
        self.dma("sp", self.gsub[:], subln_g.partition_broadcast(128), "c2", (), ("gsub",))
        self.gsubc = self.sb("gsubc", [128, 1], F32, st)
        self.dma("sp", self.gsubc[:], subln_g.rearrange("(v o) -> v o", o=1), "c3", (), ("gsubc",))
        S.add("dve", lambda e: e.tensor_scalar(out=self.gsubc[:], in0=self.gsubc[:], scalar1=1.0 - LAM_INIT,
                                               scalar2=None, op0=ALU.mult), ("gsubc",), ("gsubc",))
        S.add("act", lambda e: e.activation(out=self.slope[:], in_=hid[:], func=AF.Exp, scale=-0.5 * LN2),
              ("hid",), ("slope",))
        S.add("act", lambda e: e.activation(out=self.qs[:], in_=hid[:], func=AF.Exp, scale=0.5 * LN2),
              ("hid",), ("qs",))
        S.add("dve", lambda e: e.tensor_scalar(out=self.qs[:], in0=self.qs[:], scalar1=0.125, scalar2=None,
                                               op0=ALU.mult), ("qs",), ("qs",))
        S.add("dve", lambda e: e.memset(self.negc[:], 0.0), (), ("negc",))
        S.add("dve", lambda e: e.tensor_tensor(out=lt[:, 0, :], in0=lt[:, 0, :], in1=lt[:, 1, :], op=ALU.mult),
              ("lt",), ("lt",))
        S.add("dve", lambda e: e.tensor_tensor(out=lt[:, 2, :], in0=lt[:, 2, :], in1=lt[:, 3, :], op=ALU.mult),
              ("lt",), ("lt",))
        S.add("dve", lambda e: e.reduce_sum(out=self.lam[:, 0:1], in_=lt[:, 0, :], axis=AX.X), ("lt",), ("lam",))
        S.add("dve", lambda e: e.reduce_sum(out=self.lam[:, 1:2], in_=lt[:, 2, :], axis=AX.X), ("lam", "lt"), ("lam",))
        S.add("act", lambda e: e.activation(out=self.lam[:, 0:2], in_=self.lam[:, 0:2], func=AF.Exp),
              ("lam",), ("lam",))
        S.add("dve", lambda e: e.tensor_tensor(out=self.lam[:, 2:3], in0=self.lam[:, 0:1], in1=self.lam[:, 1:2],
                                               op=ALU.subtract), ("lam",), ("lam",))
        S.add("dve", lambda e: e.tensor_scalar(out=self.lam[:, 3:4], in0=self.lam[:, 2:3], scalar1=LAM_INIT,
                                               scalar2=-1.0, op0=ALU.add, op1=ALU.mult), ("lam",), ("lam",))
        S.add("dve", lambda e: e.tensor_scalar(out=self.gsub[:], in0=self.gsub[:], scalar1=1.0 - LAM_INIT,
                                               scalar2=None, op0=ALU.mult), ("gsub",), ("gsub",))

    def attn_phase(self, ybT, st, dbg=None):
        S = self.S
        NW = 2 * SEQ
        NKT = SEQ // 128
        LOOK = 3
        T0 = self.sb("T0", [128, NW], mybir.dt.int16, st)
        it = self.sb("iota", [128, 2048], I32, st)
        itf = self.sb("iotaf", [128, 2048], F32, st)
        tmp = [self.sb("atmp%d" % i, [128, 512], F32, st) for i in range(4)]
        PT = [self.sb("PT%d" % i, [128, 512], BF16, st) for i in range(4)]
        onb = self.sb("onb", [128, 128], BF16, st)
        onf = self.sb("onf", [128, 128], F32, st)
        rz = [self.sb("rz%d" % i, [128, 512], F32, st) for i in range(2)]
        o_ = self.sb("ao", [128, 512], F32, st)
        sq = self.sb("asq", [128, 512], F32, st)
        obf = [self.sb("aob%d" % i, [128, 512], BF16, st) for i in range(2)]
        S.add("dve", lambda e: e.memset(onb[:], 1.0), (), ("onb",))
        S.add("dve", lambda e: e.memset(onf[:], 1.0), (), ("onf",))
        for c in range(NW // 2048):
            S.add("pool", lambda e, c=c: e.iota(it[:], pattern=[[1, 2048]], base=c * 2048 - SEQ,
                                                 channel_multiplier=-1), (), ("iota",))
            S.add("dve", lambda e, c=c: e.tensor_copy(out=itf[:], in_=it[:]), ("iota",), ("iotaf",))
            S.add("dve", lambda e, c=c: e.scalar_tensor_tensor(
                out=T0[:, c * 2048:(c + 1) * 2048], in0=itf[:], scalar=-1.0,
                in1=itf[:], op0=ALU.mult, op1=ALU.min), ("iotaf",), ("T0",))
        kz = [self.sb("kz%d" % m, [128, SEQ], BF16, st) for m in range(2)]
        for m in range(2):
            z = kz[m][64 * (1 - m):64 * (1 - m) + 64, :]
            S.add("pool", lambda e, z=z: e.memset(z, 0.0), (), (("kz", m),))
        for h in range(2 if dbg is None else dbg[0]):
            for m in range(2):
                S.add("pool", lambda e, h=h, m=m: e.tensor_copy(out=kz[m][64 * m:64 * m + 64, :],
                                                                in_=self.kT[h][64 * m:64 * m + 64, :]),
                      (("kT", h),), (("kz", m),))
            for g in range(SEQ // 512 if dbg is None else dbg[1]):
                def scores(u, h=h, g=g):
                    kt, m = u // 2, u % 2
                    b = u % 4
                    S.add("pe", lambda e, b=b, m=m, kt=kt: e.matmul(
                        self.ps[b][:], lhsT=kz[m][:, kt * 128:(kt + 1) * 128],
                        rhs=self.qT[h][:, g * 512:(g + 1) * 512], start=True, stop=True),
                        (("kz", m), ("qT", h)), (("ps", b),))
                for u in range(LOOK):
                    scores(u)
                for u in range(2 * NKT):
                    if u + LOOK < 2 * NKT:
                        scores(u + LOOK)
                    kt, m = u // 2, u % 2
                    b = u % 4
                    r = u % 4
                    off = SEQ + 512 * g - 128 * kt
                    S.add("dve", lambda e, b=b, r=r, off=off: e.tensor_tensor(
                        out=tmp[r][:], in0=self.ps[b][:], in1=T0[:, off:off + 512], op=ALU.add),
                        (("ps", b), "T0"), (("atmp", r),))
                    S.add("act", lambda e, r=r, h=h: e.activation(
                        out=PT[r][:], in_=tmp[r][:], func=AF.Exp, scale=self.slope[:, h:h + 1],
                        bias=self.negc[:, h:h + 1]), (("atmp", r), "slope", "negc"), (("PT", r),))
                    S.add("pe", lambda e, m=m, r=r, h=h, kt=kt: e.matmul(
                        self.ps[4 + m][:], lhsT=self.Vx[h][:, kt, 0:128], rhs=PT[r][:],
                        start=(kt == 0), stop=(kt == NKT - 1)), (("PT", r), ("Vx", h)), (("ps", 4 + m),))
                    S.add("pe", lambda e, m=m, r=r, kt=kt: e.matmul(
                        self.ps[6 + m][:], lhsT=onb[:], rhs=PT[r][:],
                        start=(kt == 0), stop=(kt == NKT - 1)), (("PT", r), "onb"), (("ps", 6 + m),))
                for m in range(2):
                    S.add("dve", lambda e, m=m: e.reciprocal(out=rz[m][:], in_=self.ps[6 + m][:]),
                          (("ps", 6 + m),), (("rz", m),))
                S.add("dve", lambda e: e.tensor_tensor(out=o_[:], in0=self.ps[4][:], in1=rz[0][:], op=ALU.mult),
                      (("ps", 4), ("rz", 0)), ("ao",))
                S.add("dve", lambda e: e.tensor_tensor(out=rz[1][:], in0=self.ps[5][:], in1=rz[1][:], op=ALU.mult),
                      (("ps", 5), ("rz", 1)), (("rz", 1),))
                S.add("dve", lambda e: e.scalar_tensor_tensor(out=o_[:], in0=rz[1][:], scalar=self.lam[:, 3:4],
                                                              in1=o_[:], op0=ALU.mult, op1=ALU.add),
                      (("rz", 1), "lam", "ao"), ("ao",))
                S.add("act", lambda e: e.activation(out=sq[:], in_=o_[:], func=AF.Square), ("ao",), ("asq",))
                bq = (2 * NKT + LOOK) % 4
                S.add("pe", lambda e, bq=bq: e.matmul(self.ps[bq][:], lhsT=onf[:], rhs=sq[:], start=True, stop=True),
                      ("onf", "asq"), (("ps", bq),))
                S.add("act", lambda e, bq=bq: e.activation(out=sq[:], in_=self.ps[bq][:], func=AF.Sqrt,
                                                           bias=SUBLN_EPS, scale=1.0 / 128),
                      (("ps", bq),), ("asq",))
                S.add("dve", lambda e: e.reciprocal(out=sq[:], in_=sq[:]), ("asq",), ("asq",))
                ok = self.rot("aob", 2)
                S.add("dve", lambda e, ok=ok: e.scalar_tensor_tensor(
                    out=obf[ok][:], in0=o_[:], scalar=self.gsubc[:, 0:1], in1=sq[:], op0=ALU.mult, op1=ALU.mult),
                    ("ao", "asq", "gsubc"), (("aob", ok),))
                self.dma("sp", ybT[h][:, g * 512:(g + 1) * 512], obf[ok][:], ("obo", ok), (("aob", ok),), ())
                if g % 8 == 7:
                    S.emit()


def build_B(with_rwkv=True, with_attn=True, dbg=None, rdbg=None):
    B = BuilderB()
    nc = B.nc
    I = lambda n, s: nc.dram_tensor(n, list(s), F32, kind="ExternalInput").ap()
    xnT_all = nc.dram_tensor("xnT_all", [NBLK, 128, KC, NT], BF16, kind="ExternalInput").ap()
    Wb = I("Wb", [D, WB_COLS])
    hidx = I("hidx", [128, 2])
    lq1, lk1, lq2, lk2 = I("lq1", [1, 64]), I("lk1", [1, 64]), I("lq2", [1, 64]), I("lk2", [1, 64])
    subln_g = I("subln_g", [1, 128])
    yb = nc.dram_tensor("ybT", [2, 128, SEQ], BF16, kind="ExternalOutput").ap()
    PR = [B.dram("PR%d" % i, [128, SEQ + 2], F32) for i in range(10)]
    B.PR = PR
    if with_rwkv:
        B.rwkv_inputs(I)
    B.setup_common()
    st_attn = B.stack.enter_context(ExitStack())
    B.qT = [B.sb("qT%d" % h, [128, SEQ], BF16, st_attn) for h in range(2)]
    B.kT = [B.sb("kT%d" % h, [128, SEQ], BF16, st_attn) for h in range(2)]
    B.Vx = [B.sb("Vx%d" % h, [128, SEQ // 128, 128], BF16, st_attn) for h in range(2)]
    B.attn_consts(hidx, lq1[0], lk1[0], lq2[0], lk2[0], subln_g[0], st_attn)
    with ExitStack() as st:
        B.proj_phase(xnT_all, Wb, PR, st)
        B.S.emit()
    if with_attn:
        with ExitStack() as st:
            B.attn_phase(yb, st, dbg)
            B.S.emit()
    st_attn.close()
    if with_rwkv:
        B.rwkv_phase(rdbg)
    return B


def wb_for_core(w_in, c):
    R = 2048
    cols = []
    for base in (0, R, 2 * R):
        cols.append(np.arange(base + 256 * c, base + 256 * c + 256))
    cols.append(np.arange(3 * R, 3 * R + 448))
    dbase = 3 * R + 448
    for base in (0, R, 2 * R):
        cols.append(np.arange(dbase + base + 256 * c, dbase + base + 256 * c + 256))
    cols = np.concatenate(cols)
    assert cols.size == WB_COLS
    return np.ascontiguousarray(w_in[:, cols]), cols


GN_EPS = 64e-5
CDEC = -0.6065306597126334
CHK = 64
NCHK = SEQ // CHK


def _rwkv_methods():
    def o_tt(self, eng, out, in0, in1, op, R, W):
        self.S.add(eng, lambda e: e.tensor_tensor(out=out, in0=in0, in1=in1, op=op), R, W)

    def o_ts(self, eng, out, in0, s1, s2, op0, op1, R, W):
        if s2 is None:
            self.S.add(eng, lambda e: e.tensor_scalar(out=out, in0=in0, scalar1=s1, scalar2=None, op0=op0), R, W)
        else:
            self.S.add(eng, lambda e: e.tensor_scalar(out=out, in0=in0, scalar1=s1, scalar2=s2, op0=op0, op1=op1), R, W)

    def o_stt(self, eng, out, in0, sc, in1, op0, op1, R, W):
        self.S.add(eng, lambda e: e.scalar_tensor_tensor(out=out, in0=in0, scalar=sc, in1=in1, op0=op0, op1=op1), R, W)

    def o_act(self, out, in_, func, R, W, bias=None, scale=None):
        kw = {}
        if bias is not None:
            kw["bias"] = bias
        if scale is not None:
            kw["scale"] = scale
        self.S.add("act", lambda e: e.activation(out=out, in_=in_, func=func, **kw), R, W)

    def o_mm(self, out, lhsT, rhs, R, W, start=True, stop=True):
        self.S.add("pe", lambda e: e.matmul(out, lhsT=lhsT, rhs=rhs, start=start, stop=stop), R, W)

    def o_tp(self, out, in_, R, W):
        idf = self.ident_f
        n = in_.shape[0]
        self.S.add("pe", lambda e: e.transpose(out=out, in_=in_, identity=idf[0:n, 0:n]), list(R) + ["ident_f"], W)

    def o_rec(self, out, in_, R, W):
        self.S.add("dve", lambda e: e.reciprocal(out=out, in_=in_), R, W)

    def o_memset(self, eng, ap, val, W):
        self.S.add(eng, lambda e: e.memset(ap, val), (), W)

    def o_asel(self, out, in_, pattern, cmp_, cm, R, W):
        self.S.add("pool", lambda e: e.affine_select(out=out, in_=in_, pattern=pattern, compare_op=cmp_,
                                                     fill=0.0, base=0, channel_multiplier=cm), R, W)
    return dict(o_tt=o_tt, o_ts=o_ts, o_stt=o_stt, o_act=o_act, o_mm=o_mm, o_tp=o_tp, o_rec=o_rec,
                o_memset=o_memset, o_asel=o_asel)


for _k, _v in _rwkv_methods().items():
    setattr(BuilderB, _k, _v)


def _rwkv_inputs(self, I):
    nc = self.nc
    self.mu_in = I("mu", [128, 2, 10])
    self.pv_in = I("pv", [128, 9, 2])
    self.w2a2_in = I("w2a2", [96, 4, 256])
    self.g2_in = I("g2w", [128, 2, 256])
    self.yT = nc.dram_tensor("yT", [2, 128, SEQ], BF16, kind="ExternalOutput").ap()
    D_ = lambda n, s: self.dram(n, s, F32)
    self.FM = [[D_("FM%d%d" % (ch, d), [4, 128, SEQ]) for d in range(2)] for ch in range(2)]
    self.TM = [[D_("TM%d%d" % (ch, d), [3, SEQ, 128]) for d in range(2)] for ch in range(2)]
    self.VT = [D_("VT%d" % ch, [SEQ, 128]) for ch in range(2)]
    self.BN = [D_("BN%d" % ch, [128, SEQ]) for ch in range(2)]
    self.GT = [D_("GT%d" % ch, [128, SEQ]) for ch in range(2)]
    self.YO = [[D_("YO%d%d" % (ch, d), [SEQ, 128]) for d in range(2)] for ch in range(2)]


def _rwkv_consts(self, st):
    f = lambda n, s: self.sb(n, s, F32, st)
    self.BO = f("BO", [128, 128])
    self.BO64 = f("BO64", [128, 128])
    self.ones = f("ones", [128, 128])
    self.TRI = {k: f("TRI" + k, [128, 128]) for k in ("Li", "Ls", "Us", "Ui")}
    self.MT = [f("MT%d" % d, [128, 128]) for d in range(2)]
    self.MAB = [f("MAB%d" % d, [64, 64]) for d in range(2)]
    self.mu = f("mu_sb", [128, 2, 10])
    self.c0 = f("c0_sb", [128, 10])
    self.pv = f("pv_sb", [128, 9, 2])
    self.w2a2 = self.sb("w2a2_sb", [96, 4, 256], F32, st)
    self.g2 = f("g2_sb", [128, 2, 256])
    self.gc = [[f("gc%d%d" % (ch, d), [128, NCHK]) for d in range(2)] for ch in range(2)]
    self.S0T = [[f("S0T%d%d" % (ch, d), [128, 64]) for d in range(2)] for ch in range(2)]
    BO, BO64, ones = self.BO, self.BO64, self.ones
    self.o_memset("pool", ones[:], 1.0, ("ones",))
    self.o_memset("pool", BO[:], 0.0, ("BO",))
    self.o_memset("pool", BO[0:64, 0:64], 1.0, ("BO",))
    self.o_memset("pool", BO[64:128, 64:128], 1.0, ("BO",))
    self.o_ts("pool", BO64[:], BO[:], 1.0 / 64, None, ALU.mult, None, ("BO",), ("BO64",))
    for k, (pat, cm, cmp_) in {"Li": ([[1, 128]], -1, ALU.is_ge), "Ls": ([[1, 128]], -1, ALU.is_gt),
                               "Us": ([[-1, 128]], 1, ALU.is_gt), "Ui": ([[-1, 128]], 1, ALU.is_ge)}.items():
        T_ = self.TRI[k]
        self.o_ts("pool", T_[:], BO[:], CDEC, None, ALU.mult, None, ("BO",), (("TRI", k),))
        self.o_asel(T_[:], T_[:], pat, cmp_, cm, (("TRI", k),), (("TRI", k),))
    for d in range(2):
        M = self.MT[d]
        for r0 in (0, 64):
            for c0_, strict in ((0, True), (64, False)):
                if d == 0:
                    pat, cm = [[1, 64]], -1
                else:
                    pat, cm = [[-1, 64]], 1
                self.o_asel(M[r0:r0 + 64, c0_:c0_ + 64], ones[r0:r0 + 64, 0:64], pat,
                            ALU.is_gt if strict else ALU.is_ge, cm, ("ones",), (("MT", d),))
        pat, cm = ([[-1, 64]], 1) if d == 0 else ([[1, 64]], -1)
        self.o_asel(self.MAB[d][:], ones[0:64, 0:64], pat, ALU.is_gt, cm, ("ones",), (("MAB", d),))
    self.dma("sp", self.mu[:], self.mu_in, "rc0", (), ("mu",))
    self.dma("sp", self.pv[:], self.pv_in, "rc1", (), ("pv",))
    self.dma("sp", self.w2a2[:], self.w2a2_in, "rc2", (), ("w2a2",))
    self.dma("sp", self.g2[:], self.g2_in, "rc3", (), ("g2",))
    self.o_tt("dve", self.c0[:], self.mu[:, 0, :], self.mu[:, 1, :], ALU.add, ("mu",), ("c0",))
    self.o_ts("dve", self.c0[:], self.c0[:], -1.0, 1.0, ALU.mult, ALU.add, ("c0",), ("c0",))
    for ch in range(2):
        for d in range(2):
            self.o_memset("dve", self.S0T[ch][d][:], 0.0, (("S0T", ch, d),))


def _rwkv_prep(self, st, ntiles=None):
    S = self.S
    TB = 512
    f = lambda n, s=(128, TB): self.sb(n, list(s), F32, st)
    PERCH = ("sigw0", "sigw1", "aic0", "aic1", "kd0", "kd1", "bd0", "bd1", "gsb", "kk", "kk2", "rn", "kkn", "tmpa",
             "ksum", "rk", "bon", "g_incl", "g_excl", "g_inv", "g_sfx", "ahat", "rhat", "bhat", "khat", "btil", "ktil")
    sets = [{n: f("%s_c%d" % (n, c)) for n in PERCH} for c in range(2)]
    sigT2 = [f("sigT_c%d" % c, (128, 4, 128)) for c in range(2)]
    xT2 = [[f("xTt%d_c%d" % (i, c), (128, 4, 128)) for i in range(2)] for c in range(2)]
    raw = [f("raw%d" % rc, (128, TB + 2)) for rc in range(10)]
    sh = [f("sh%d" % rc) for rc in range(10)]
    hw, sg0, sg1 = f("hw"), f("sg0"), f("sg1")
    cur = {}
    seg_marks = []
    pv, mu, c0 = self.pv, self.mu, self.c0

    def bank():
        c_ = cur.get("ch", 0)
        b = 4 * c_ + self.rot("r1ps%d" % c_, 4)
        return b, self.ps[b]

    def to_tokmajor(src, src_key, dst_dram, dst_key):
        b, p = bank()
        for j in range(4):
            self.o_tp(p[:, j * 128:(j + 1) * 128], src[:, j * 128:(j + 1) * 128], (src_key,), (("ps", b),))
        k = self.rot("xTt", 2)
        xT_ = cur["xT"]
        self.evac_copy(xT_[k][:].rearrange("p j c -> p (j c)"), p, (("ps", b),), (("xTt", k),))
        self.dma("act", dst_dram.rearrange("(j p) c -> p j c", p=128), xT_[k][:], ("xTto", k, cur["ch"]),
                 (("xTt", k),), (dst_key,))

    for tb in range(SEQ // TB if ntiles is None else ntiles):
        t0 = tb * TB
        for rc in range(10):
            rows = RC_ROWS[rc]
            self.dma("sp", raw[rc][0:rows, :], self.PR[rc][0:rows, t0:t0 + TB + 2], ("raw", rc),
                     [("PR", rc, b_) for b_ in range(NBLK)] + [("PRz", rc)], (("raw", rc),))
            r_, s_ = raw[rc], sh[rc]
            self.o_ts("dve", s_[0:rows, :], r_[0:rows, 1:TB + 1], c0[0:rows, rc:rc + 1], None, ALU.mult, None,
                      (("raw", rc), "c0"), (("sh", rc),))
            self.o_stt("dve", s_[0:rows, :], r_[0:rows, 0:TB], mu[0:rows, 0, rc:rc + 1], s_[0:rows, :], ALU.mult,
                       ALU.add, (("raw", rc), "mu", ("sh", rc)), (("sh", rc),))
            self.o_stt("dve", s_[0:rows, :], r_[0:rows, 2:TB + 2], mu[0:rows, 1, rc:rc + 1], s_[0:rows, :], ALU.mult,
                       ALU.add, (("raw", rc), "mu", ("sh", rc)), (("sh", rc),))
        self.o_act(hw[0:96, :], sh[6][0:96, :], AF.Tanh, (("sh", 6),), ("hw",))
        self.o_act(sg0[:], sh[8][:], AF.Sigmoid, (("sh", 8),), ("sg0",))
        self.o_act(sg1[:], sh[9][:], AF.Sigmoid, (("sh", 9),), ("sg1",))
        for ch in range(2):
            T_ = sets[ch]
            sigw, aa, kd, bd = [T_["sigw0"], T_["sigw1"]], [T_["aic0"], T_["aic1"]], [T_["kd0"], T_["kd1"]], [T_["bd0"], T_["bd1"]]
            gsb, kk, kk2, rn, kkn, tmpa, ksum, rk, bon = [T_[n] for n in
                                                          ("gsb", "kk", "kk2", "rn", "kkn", "tmpa", "ksum", "rk", "bon")]
            gi, ge, gv, gs = T_["g_incl"], T_["g_excl"], T_["g_inv"], T_["g_sfx"]
            ahat, rhat, bhat, khat, btil, ktil = [T_[n] for n in ("ahat", "rhat", "bhat", "khat", "btil", "ktil")]
            sigT = sigT2[ch]
            cur["xT"], cur["ch"] = xT2[ch], ch
            S.ktag = ch
            seg_marks.append(len(S.ops))
            cs = slice(ch * 128, (ch + 1) * 128)
            r_s, k_s, v_s = sh[ch], sh[2 + ch], sh[4 + ch]
            rK, kK, vK = ("sh", ch), ("sh", 2 + ch), ("sh", 4 + ch)
            for d in range(2):
                b, p = bank()
                self.o_mm(p, self.w2a2[:, d, cs], hw[0:96, :], ("w2a2", "hw"), (("ps", b),))
                self.o_act(sigw[d][:], p, AF.Sigmoid, (("ps", b), "pv"), (("sigw", d),), bias=pv[:, 5 + d, ch:ch + 1])
                b, p = bank()
                self.o_mm(p, self.w2a2[:, 2 + d, cs], sh[7][0:96, :], ("w2a2", ("sh", 7)), (("ps", b),))
                self.o_act(aa[d][:], p, AF.Sigmoid, (("ps", b), "pv"), (("aic", d),), bias=pv[:, 7 + d, ch:ch + 1])
            b, p = bank()
            self.o_mm(p, self.g2[:, 0, cs], sg0[:], ("g2", "sg0"), (("ps", b),), start=True, stop=False)
            self.o_mm(p, self.g2[:, 1, cs], sg1[:], ("g2", "sg1"), (("ps", b),), start=False, stop=True)
            self.evac_copy(gsb[:], p, (("ps", b),), ("gsb",))
            self.dma("act", self.GT[ch][:, t0:t0 + TB], gsb[:], "gto", ("gsb",), (("GT", ch, tb),))
            self.o_ts("dve", kk[:], k_s[:], pv[:, 0, ch:ch + 1], None, ALU.mult, None, (kK, "pv"), ("kk",))
            self.o_tt("dve", kk2[:], kk[:], kk[:], ALU.mult, ("kk",), ("kk2",))
            b, p = bank()
            self.o_mm(p, self.BO[:], kk2[:], ("BO", "kk2"), (("ps", b),))
            self.o_act(rn[:], p, AF.Sqrt, (("ps", b),), ("rn",))
            self.o_ts("dve", rn[:], rn[:], 1e-12, None, ALU.max, None, ("rn",), ("rn",))
            self.o_rec(rn[:], rn[:], ("rn",), ("rn",))
            self.o_tt("dve", kkn[:], kk[:], rn[:], ALU.mult, ("kk", "rn"), ("kkn",))
            for d in range(2):
                self.o_ts("dve", tmpa[:], aa[d][:], -1.0, pv[:, 1, ch:ch + 1], ALU.add, ALU.mult,
                          (("aic", d), "pv"), ("tmpa",))
                self.o_stt("dve", kd[d][:], tmpa[:], 1.0, k_s[:], ALU.add, ALU.mult, ("tmpa", kK), (("kd", d),))
                self.o_tt("dve", bd[d][:], kkn[:], aa[d][:], ALU.mult, ("kkn", ("aic", d)), (("bd", d),))
            self.o_tt("dve", ksum[:], kd[0][:], kd[1][:], ALU.add, (("kd", 0), ("kd", 1)), ("ksum",))
            self.o_stt("dve", rk[:], r_s[:], pv[:, 2, ch:ch + 1], ksum[:], ALU.mult, ALU.mult, (rK, "pv", "ksum"), ("rk",))
            b, p = bank()
            self.o_mm(p, self.BO[:], rk[:], ("BO", "rk"), (("ps", b),))
            self.o_tt("dve", bon[:], p, v_s[:], ALU.mult, (("ps", b), vK), ("bon",))
            self.dma("act", self.BN[ch][:, t0:t0 + TB], bon[:], "bno", ("bon",), (("BN", ch, tb),))
            to_tokmajor(v_s, vK, self.VT[ch][t0:t0 + TB, :], ("VT", ch, tb))
            for d in range(2):
                b, p = bank()
                for j in range(4):
                    self.o_tp(p[:, j * 128:(j + 1) * 128], sigw[d][:, j * 128:(j + 1) * 128], (("sigw", d),), (("ps", b),))
                self.evac_copy(sigT[:].rearrange("p j c -> p (j c)"), p, (("ps", b),), ("sigT",))
                tri = ("Li", "Ls", "Us") if d == 0 else ("Ui", "Us", "Ls")
                banks = []
                for k_ in tri:
                    b, p = bank()
                    banks.append((b, p))
                    for j in range(4):
                        self.o_mm(p[:, j * 128:(j + 1) * 128], sigT[:, j, :], self.TRI[k_][:],
                                  ("sigT", ("TRI", k_)), (("ps", b),))
                (bi, pi), (be, pe_), (bs, ps_) = banks
                self.o_act(gi[:], pi, AF.Exp, (("ps", bi),), ("gi",))
                self.o_act(gv[:], pi, AF.Exp, (("ps", bi),), ("gv",), scale=-1.0)
                self.o_act(ge[:], pe_, AF.Exp, (("ps", be),), ("ge",))
                self.o_act(gs[:], ps_, AF.Exp, (("ps", bs),), ("gs",))
                last = CHK - 1 if d == 0 else 0
                gsrc = gi[:].rearrange("p (c t) -> p c t", t=CHK)[:, :, last]
                self.o_ts("dve", self.gc[ch][d][:, tb * 8:(tb + 1) * 8], gsrc, 1.0, None, ALU.mult, None,
                          ("gi",), (("gc", ch, d),))
                self.o_stt("dve", ahat[:], kkn[:], -1.0, ge[:], ALU.mult, ALU.mult, ("kkn", "ge"), ("ahat",))
                self.o_tt("dve", rhat[:], r_s[:], gi[:], ALU.mult, (rK, "gi"), ("rhat",))
                self.o_tt("dve", bhat[:], bd[d][:], gv[:], ALU.mult, (("bd", d), "gv"), ("bhat",))
                self.o_tt("dve", khat[:], kd[d][:], gv[:], ALU.mult, (("kd", d), "gv"), ("khat",))
                self.o_tt("dve", btil[:], bd[d][:], gs[:], ALU.mult, (("bd", d), "gs"), ("btil",))
                self.o_tt("dve", ktil[:], kd[d][:], gs[:], ALU.mult, (("kd", d), "gs"), ("ktil",))
                for xi, (t_, nm) in enumerate(((ahat, "ahat"), (rhat, "rhat"), (bhat, "bhat"), (khat, "khat"))):
                    self.dma("sp", self.FM[ch][d][xi][:, t0:t0 + TB], t_[:], ("fmo", xi), (nm,), (("FM", ch, d, tb),))
                for xi, (t_, nm) in enumerate(((ahat, "ahat"), (btil, "btil"), (ktil, "ktil"))):
                    to_tokmajor(t_, nm, self.TM[ch][d][xi][t0:t0 + TB, :], ("TM", ch, d, tb))
        S.ktag = None
        a_, b_ = seg_marks[-2], seg_marks[-1]
        s0, s1 = S.ops[a_:b_], S.ops[b_:]
        merged = []
        for k_ in range(max(len(s0), len(s1))):
            if k_ < len(s0):
                merged.append(s0[k_])
            if k_ < len(s1):
                merged.append(s1[k_])
        S.ops[a_:] = merged
        if tb % 4 == 3:
            S.emit()
    S.emit()


BuilderB.rwkv_inputs = _rwkv_inputs
BuilderB.rwkv_consts = _rwkv_consts
BuilderB.rwkv_prep = _rwkv_prep


def _rwkv_scan(self, st, nsteps=None):
    S = self.S
    f = lambda n, s: self.sb(n, list(s), F32, st)
    CD = [(ch, d) for ch in range(2) for d in range(2)]
    INST = [(ch, d, hh) for ch in range(2) for d in range(2) for hh in range(2)]
    G = 8
    ar = {cd: f("ar%d%d" % cd, (128, G, 2, CHK)) for cd in CD}
    bkf = {cd: f("bkf%d%d" % cd, (128, G, 2, CHK)) for cd in CD}
    BK = {cd: f("BK%d%d" % cd, (128, G, 128)) for cd in CD}
    AT_ = {cd: f("ATt%d%d" % cd, (64, G, 128)) for cd in CD}
    UV = {cd: f("UV%d%d" % cd, (128, G, 128)) for cd in CD}
    Ysb = {cd: f("Ysb%d%d" % cd, (64, G, 128)) for cd in CD}
    ATm = {i: [self.sb("ATm%d%d%d_%d" % (i + (v,)), [128, 128], BF16, st) for v in range(2)] for i in INST}
    fb = lambda n, s: self.sb(n, list(s), BF16, st)
    Zp = {i: [fb("Zp%d%d%d_%d" % (i + (v,)), (64, 64)) for v in range(2)] for i in INST}
    Ztp = {i: [fb("Ztp%d%d%d_%d" % (i + (v,)), (64, 64)) for v in range(2)] for i in INST}
    NTp = {i: [fb("NTp%d%d%d_%d" % (i + (v,)), (64, 64)) for v in range(2)] for i in INST}
    Z0b = {i: fb("Z0b%d%d%d" % i, (64, 64)) for i in INST}
    Wsb = {i: fb("Wsb%d%d%d" % i, (64, 64)) for i in INST}
    ATb = {cd: fb("ATb%d%d" % cd, (64, G, 128)) for cd in CD}
    arb = {cd: fb("arb%d%d" % cd, (128, G, 2, CHK)) for cd in CD}
    bkfb = {cd: fb("bkfb%d%d" % cd, (128, G, 2, CHK)) for cd in CD}
    BKb = {cd: fb("BKb%d%d" % cd, (128, G, 128)) for cd in CD}
    UVb = {cd: fb("UVb%d%d" % cd, (128, G, 128)) for cd in CD}
    S0b = {cd: fb("S0b%d%d" % cd, (128, 64)) for cd in CD}
    for cd in CD:
        self.o_memset("dve", S0b[cd][:], 0.0, (("S0b", cd, 0), ("S0b", cd, 1)))
    Xsb = {i: [f("Xsb%d%d%d_%d" % (i + (v,)), (64, 64)) for v in range(2)] for i in INST}
    PTs = {i: [self.sb("PTs%d%d%d_%d" % (i + (v,)), [128, 64], BF16, st) for v in range(2)] for i in INST}
    idf = self.ident_f

    def region():
        q = self.rot("psr", 32)
        b, sl_ = q % 8, q // 8
        return ("ps", b), self.ps[b][:, sl_ * 128:(sl_ + 1) * 128]

    nst = NCHK if nsteps is None else nsteps
    for n in range(nst):
        v = n % 2
        grp, cc = n // G, n % G
        CI = {}
        for (ch, d) in CD:
            tb = grp if d == 0 else SEQ // 512 - 1 - grp
            c = cc if d == 0 else G - 1 - cc
            CI[(ch, d)] = (c, tb * G + c)
            if cc != 0:
                continue
            cd = (ch, d)
            t0 = tb * 512
            fmk, tmk = ("FM", ch, d, tb), ("TM", ch, d, tb)
            FMd, TMd = self.FM[ch][d], self.TM[ch][d]
            fmv = lambda x: FMd[x][:, t0:t0 + 512].rearrange("p (c t) -> p c t", t=CHK)
            tmv = lambda x: TMd[x][t0:t0 + 512, :].rearrange("(c s) k -> s c k", s=CHK)
            self.dma("sp", ar[cd][:, :, 0, :], fmv(0), ("l_ar", cd), (fmk,), (("ar", cd),))
            self.dma("sp", ar[cd][:, :, 1, :], fmv(1), ("l_ar", cd), (fmk,), (("ar", cd),))
            self.dma("sp", bkf[cd][:, :, 0, :], fmv(2), ("l_bkf", cd), (fmk,), (("bkf", cd),))
            self.dma("sp", bkf[cd][:, :, 1, :], fmv(3), ("l_bkf", cd), (fmk,), (("bkf", cd),))
            self.dma("sp", AT_[cd][:], tmv(0), ("l_at", cd), (tmk,), (("AT_", cd),))
            self.S.add("act", lambda e, o=ATb[cd][:], i_=AT_[cd][:]: e.copy(out=o, in_=i_),
                       (("AT_", cd),), (("ATb", cd),))
            self.dma("sp", BK[cd][0:64, :, :], tmv(1), ("l_bk", cd), (tmk,), (("BK", cd),))
            self.dma("sp", BK[cd][64:128, :, :], tmv(2), ("l_bk", cd), (tmk,), (("BK", cd),))
            self.dma("sp", UV[cd][64:128, :, :],
                     self.VT[ch][t0:t0 + 512, :].rearrange("(c s) k -> s c k", s=CHK), ("l_uv", cd),
                     (("VT", ch, tb),), (("UVv", cd),))
            self.S.add("act", lambda e, o=arb[cd][:], i_=ar[cd][:]: e.copy(out=o, in_=i_), (("ar", cd),), (("arb", cd),))
            self.S.add("dve", lambda e, o=bkfb[cd][:], i_=bkf[cd][:]: e.tensor_copy(out=o, in_=i_),
                       (("bkf", cd),), (("bkfb", cd),))
            self.S.add("act", lambda e, o=BKb[cd][:], i_=BK[cd][:]: e.copy(out=o, in_=i_), (("BK", cd),), (("BKb", cd),))
            self.S.add("dve", lambda e, o=UVb[cd][64:128, :, :], i_=UV[cd][64:128, :, :]: e.tensor_copy(out=o, in_=i_),
                       (("UVv", cd),), (("UVbv", cd),))
        reg = {}
        for i in INST:
            ch, d, hh = i
            cd, hp = (ch, d), 64 * hh
            k1, p1 = region()
            self.o_mm(p1, bkfb[cd][hp:hp + 64, CI[cd][0], :, :], arb[cd][hp:hp + 64, CI[cd][0], :, :],
                      (("bkfb", cd), ("arb", cd)), (k1,))
            k2, p2 = region()
            self.o_mm(p2[0:64, 0:64], arb[cd][hp:hp + 64, CI[cd][0], 0, :], bkfb[cd][hp:hp + 64, CI[cd][0], 0, :],
                      (("bkfb", cd), ("arb", cd)), (k2,))
            reg[i] = (k1, p1, k2, p2)
        for i in INST:
            ch, d, hh = i
            k1, p1, k2, p2 = reg[i]
            self.o_tt("dve", ATm[i][v][:], p1, self.MT[d][:], ALU.mult, (k1, ("MT", d)), (("ATm", i, v),))
            self.o_tt("dve", Ztp[i][0][:], p2[0:64, 0:64], self.MAB[d][:], ALU.mult, (k2, ("MAB", d)), (("Ztp", i, 0),))
            self.o_tt("pool", NTp[i][0][:], ATm[i][v][0:64, 0:64], idf[0:64, 0:64], ALU.add,
                      (("ATm", i, v), "ident_f"), (("NTp", i, 0),))
            self.o_tt("dve", Z0b[i][:], p1[0:64, 0:64], self.MT[d][0:64, 0:64], ALU.mult, (k1, ("MT", d)),
                      (("Z0b", i),))
        for j in range(1, 6):
            a, b_ = (j - 1) % 2, j % 2
            for i in INST:
                Zc = Z0b[i][:] if j == 1 else Zp[i][a][:]
                Zck = ("Z0b", i) if j == 1 else ("Zp", i, a)
                Ztc, Ztck = Ztp[i][a][:], ("Ztp", i, a)
                if j < 5:
                    k1, p1 = region()
                    self.o_mm(p1[0:64, 0:64], Ztc, Zc, (Zck, Ztck), (k1,))
                else:
                    k1, p1 = None, None
                k2, p2 = region()
                self.o_mm(p2[0:64, 0:64], Zc, Ztc, (Zck, Ztck), (k2,))
                reg[i] = (k1, p1, k2, p2)
            for i in INST:
                k1, p1, k2, p2 = reg[i]
                if k1 is not None:
                    self.S.add("act", lambda e, o=Zp[i][b_][:], s=p1[0:64, 0:64]: e.copy(out=o, in_=s),
                               (k1,), (("Zp", i, b_),))
                self.S.add("dve", lambda e, o=Ztp[i][b_][:], s=p2[0:64, 0:64]: e.tensor_copy(out=o, in_=s),
                           (k2,), (("Ztp", i, b_),))
            for i in INST:
                k3, p3 = region()
                self.o_mm(p3[0:64, 0:64], Ztp[i][b_][:], NTp[i][a][:], (("Ztp", i, b_), ("NTp", i, a)), (k3,))
                reg[i] = (k3, p3)
            for i in INST:
                k3, p3 = reg[i]
                self.o_tt("dve", NTp[i][b_][:], p3[0:64, 0:64], NTp[i][a][:], ALU.add, (k3, ("NTp", i, a)),
                          (("NTp", i, b_),))
        NTf = 1
        for i in INST:
            ch, d, hh = i
            cd, hk = (ch, d), slice(64 * hh, 64 * hh + 64)
            k1, p1 = region()
            self.o_mm(p1[0:64, 0:64], ATm[i][v][64:128, 0:64], UVb[cd][64:128, CI[cd][0], hk],
                      (("ATm", i, v), ("UVbv", cd)), (k1,))
            reg[i] = (k1, p1)
        for i in INST:
            k1, p1 = reg[i]
            self.S.add("act", lambda e, o=Wsb[i][:], s=p1[0:64, 0:64]: e.copy(out=o, in_=s), (k1,), (("Wsb", i),))
        for i in INST:
            ch, d, hh = i
            cd, hp, hk = (ch, d), 64 * hh, slice(64 * hh, 64 * hh + 64)
            k1, p1 = region()
            self.o_mm(p1[0:64, 0:64], NTp[i][NTf][:], Wsb[i][:], (("NTp", i, NTf), ("Wsb", i)), (k1,))
            k2, p2 = region()
            self.o_mm(p2[hp:hp + 64, 0:64], ATb[cd][0:64, CI[cd][0], hk], NTp[i][NTf][:],
                      (("ATb", cd), ("NTp", i, NTf)), (k2,))
            reg[i] = (k1, p1, k2, p2)
        for i in INST:
            ch, d, hh = i
            hp = 64 * hh
            k1, p1, k2, p2 = reg[i]
            self.S.add("act", lambda e, o=Xsb[i][v][:], s=p1[0:64, 0:64]: e.copy(out=o, in_=s), (k1,), (("Xsb", i, v),))
            self.S.add("dve", lambda e, o=PTs[i][v][hp:hp + 64, :], s=p2[hp:hp + 64, 0:64]: e.tensor_copy(out=o, in_=s),
                       (k2,), (("PTs", i, v),))
        for i in INST:
            ch, d, hh = i
            cd, hp = (ch, d), 64 * hh
            k1, p1 = region()
            self.o_mm(p1[0:64, 0:64], PTs[i][v][hp:hp + 64, :], S0b[cd][hp:hp + 64, :],
                      (("PTs", i, v), ("S0b", cd, hh)), (k1,))
            reg[i] = (k1, p1)
        for i in INST:
            ch, d, hh = i
            cd, hk = (ch, d), slice(64 * hh, 64 * hh + 64)
            k1, p1 = reg[i]
            self.o_tt("dve", UVb[cd][0:64, CI[cd][0], hk], p1[0:64, 0:64], Xsb[i][v][:], ALU.add, (k1, ("Xsb", i, v)),
                      (("UVu", cd, hh),))
        for i in INST:
            ch, d, hh = i
            cd, hp, hk = (ch, d), 64 * hh, slice(64 * hh, 64 * hh + 64)
            uvk = (("UVu", cd, hh), ("UVbv", cd))
            k1, p1 = region()
            self.o_mm(p1[0:64, 0:64], arb[cd][hp:hp + 64, CI[cd][0], 1, :], S0b[cd][hp:hp + 64, :],
                      (("arb", cd), ("S0b", cd, hh)), (k1,), start=True, stop=False)
            self.o_mm(p1[0:64, 0:64], ATm[i][v][:, 64:128], UVb[cd][:, CI[cd][0], hk], (("ATm", i, v),) + uvk, (k1,),
                      start=False, stop=True)
            k2, p2 = region()
            self.o_mm(p2[hp:hp + 64, 0:64], BKb[cd][:, CI[cd][0], hk], UVb[cd][:, CI[cd][0], hk], (("BKb", cd),) + uvk, (k2,))
            reg[i] = (k1, p1, k2, p2)
        for i in INST:
            ch, d, hh = i
            cd, hp, hk = (ch, d), 64 * hh, slice(64 * hh, 64 * hh + 64)
            cn = CI[cd][1]
            k1, p1, k2, p2 = reg[i]
            self.S.add("act", lambda e, o=Ysb[cd][:, CI[cd][0], hk], s=p1[0:64, 0:64]: e.copy(out=o, in_=s),
                       (k1,), (("Ysb", cd, hh),))
            st_ = self.S0T[ch][d][hp:hp + 64, :]
            self.o_stt("dve", st_, st_, self.gc[ch][d][hp:hp + 64, cn:cn + 1], p2[hp:hp + 64, 0:64], ALU.mult, ALU.add,
                       (("S0T", ch, d, hh), ("gc", ch, d), k2), (("S0T", ch, d, hh),))
            self.S.add("act", lambda e, o=S0b[cd][hp:hp + 64, :], s_=st_: e.copy(out=o, in_=s_),
                       (("S0T", ch, d, hh),), (("S0b", cd, hh),))
        if cc == G - 1 or n == nst - 1:
            for (ch, d) in CD:
                cd = (ch, d)
                tb = grp if d == 0 else SEQ // 512 - 1 - grp
                t0 = tb * 512
                self.dma("act", self.YO[ch][d][t0:t0 + 512, :].rearrange("(c s) k -> s c k", s=CHK), Ysb[cd][:],
                         ("yo", cd), (("Ysb", cd, 0), ("Ysb", cd, 1)), (("YO", ch, d, tb),))
        if n % 8 == 7:
            S.emit()
    S.emit()


def _rwkv_final(self, st, ntiles=None):
    S = self.S
    TB = 512
    f = lambda n, s=(128, TB): self.sb(n, list(s), F32, st)
    yf, ybk = f("yf", (128, 4, 128)), f("ybk", (128, 4, 128))
    ysb, dd, d2, rs, bn, gt = f("ysb"), f("dd"), f("d2"), f("rs"), f("bn"), f("gt")
    ob = self.sb("rob", [128, TB], BF16, st)
    pv = self.pv
    for ch in range(2):
        for tb in range(SEQ // TB if ntiles is None else ntiles):
            t0 = tb * TB
            self.dma("sp", yf[:], self.YO[ch][0][t0:t0 + TB, :].rearrange("(j p) c -> p j c", p=128), "r3a",
                     (("YO", ch, 0, tb),), ("yf",))
            self.dma("sp", ybk[:], self.YO[ch][1][t0:t0 + TB, :].rearrange("(j p) c -> p j c", p=128), "r3b",
                     (("YO", ch, 1, tb),), ("ybk",))
            self.dma("sp", bn[:], self.BN[ch][:, t0:t0 + TB], "r3c", (("BN", ch, tb),), ("bn",))
            self.dma("sp", gt[:], self.GT[ch][:, t0:t0 + TB], "r3d", (("GT", ch, tb),), ("gt",))
            self.o_tt("dve", yf[:], yf[:], ybk[:], ALU.add, ("yf", "ybk"), ("yf",))
            b = self.rot("r3ps", 8)
            p = self.ps[b]
            for j in range(4):
                self.o_tp(p[:, j * 128:(j + 1) * 128], yf[:, j, :], ("yf",), (("ps", b),))
            self.evac_copy(ysb[:], p, (("ps", b),), ("ysb",))
            b = self.rot("r3ps", 8)
            p = self.ps[b]
            self.o_mm(p, self.BO64[:], ysb[:], ("BO64", "ysb"), (("ps", b),))
            self.o_tt("dve", dd[:], ysb[:], p, ALU.subtract, ("ysb", ("ps", b)), ("dd",))
            self.o_tt("dve", d2[:], dd[:], dd[:], ALU.mult, ("dd",), ("d2",))
            b = self.rot("r3ps", 8)
            p = self.ps[b]
            self.o_mm(p, self.BO64[:], d2[:], ("BO64", "d2"), (("ps", b),))
            self.o_act(rs[:], p, AF.Sqrt, (("ps", b),), ("rs",), bias=GN_EPS)
            self.o_rec(rs[:], rs[:], ("rs",), ("rs",))
            self.o_tt("dve", dd[:], dd[:], rs[:], ALU.mult, ("dd", "rs"), ("dd",))
            self.o_ts("dve", dd[:], dd[:], pv[:, 3, ch:ch + 1], pv[:, 4, ch:ch + 1], ALU.mult, ALU.add,
                      ("dd", "pv"), ("dd",))
            self.o_tt("dve", dd[:], dd[:], bn[:], ALU.add, ("dd", "bn"), ("dd",))
            self.o_tt("dve", ob[:], dd[:], gt[:], ALU.mult, ("dd", "gt"), ("rob",))
            self.dma("act", self.yT[ch][:, t0:t0 + TB], ob[:], "r3o", ("rob",), ())
    S.emit()


def _rwkv_phase(self, dbg=None):
    with ExitStack() as stc:
        self.rwkv_consts(stc)
        with ExitStack() as st:
            self.rwkv_prep(st, None if dbg is None else dbg[0])
        with ExitStack() as st:
            self.rwkv_scan(st, None if dbg is None else dbg[1])
        with ExitStack() as st:
            self.rwkv_final(st, None if dbg is None else dbg[2])


BuilderB.rwkv_scan = _rwkv_scan
BuilderB.rwkv_final = _rwkv_final
BuilderB.rwkv_phase = _rwkv_phase


def rwkv_params_for_core(inp, c):
    sl = slice(256 * c, 256 * c + 256)
    R = 2048
    colidx = np.zeros((10, 128), np.int64)
    valid = np.zeros((10, 128), bool)
    for rc in range(6):
        colidx[rc] = (rc // 2) * R + 256 * c + (rc % 2) * 128 + np.arange(128)
        valid[rc] = True
    colidx[6, :96] = 3 * R + np.arange(96); valid[6, :96] = True
    colidx[7, :96] = 3 * R + 96 + np.arange(96); valid[7, :96] = True
    for rc in (8, 9):
        colidx[rc] = 3 * R + 192 + (rc - 8) * 128 + np.arange(128); valid[rc] = True
    mu = np.zeros((128, 2, 10), np.float32)
    for i, nm in enumerate(("mu_prev", "mu_next")):
        m = inp[nm][0][colidx]
        mu[:, i, :] = np.where(valid, m, 0).T
    names = ("k_k", "k_a", "r_k", "gn_w", "gn_b", "w0_f", "w0_b", "a0_f", "a0_b")
    pv = np.zeros((128, 9, 2), np.float32)
    for i, nm in enumerate(names):
        vec = inp[nm][0].reshape(-1)[sl]
        pv[:, i, :] = vec.reshape(2, 128).T
    w2a2 = np.stack([inp[nm][0][:, sl] for nm in ("w2_f", "w2_b", "a2_f", "a2_b")], axis=1)
    g2w = inp["g2"][0][:, sl].reshape(2, 128, 256).transpose(1, 0, 2)
    return {"mu": np.ascontiguousarray(mu), "pv": np.ascontiguousarray(pv),
            "w2a2": np.ascontiguousarray(w2a2.astype(np.float32)), "g2w": np.ascontiguousarray(g2w.astype(np.float32))}


def kernel(**inp):
    cores = list(range(NCORES))
    x = np.ascontiguousarray(inp["x"][0])
    f32 = lambda a: np.ascontiguousarray(a, dtype=np.float32)
    A = build_A()
    shared = {"g_pre": f32(inp["ffn1_pre_g"]), "g_post": f32(inp["ffn1_post_g"]), "g_next": f32(inp["mix_pre_g"]),
              "wg": f32(inp["ffn1_w_gate"][0]), "wu": f32(inp["ffn1_w_up"][0]), "wd": f32(inp["ffn1_w_down"][0])}
    ims = [dict(shared, x=x[c * NT:(c + 1) * NT]) for c in cores]
    rA = run_bass_kernel_spmd(A.nc, ims, core_ids=cores).results
    h1 = [rA[c]["h1"] for c in cores]
    xnT_all = np.ascontiguousarray(np.stack([rA[c]["xn2T"] for c in cores], axis=0))
    del A, ims, rA
    Bp = build_B()
    w_in = inp["w_in"][0]
    ims = []
    for c in cores:
        Wb, _ = wb_for_core(w_in, c)
        im = {"xnT_all": xnT_all, "Wb": Wb,
              "hidx": np.tile(np.array([[2 * c + 1, 2 * c + 2]], np.float32), (128, 1)),
              "lq1": f32(inp["lq1"]), "lk1": f32(inp["lk1"]), "lq2": f32(inp["lq2"]), "lk2": f32(inp["lk2"]),
              "subln_g": f32(inp["subln_g"])}
        im.update(rwkv_params_for_core(inp, c))
        ims.append(im)
    rB = run_bass_kernel_spmd(Bp.nc, ims, core_ids=cores).results
    mixT_full = np.empty((128, KC, SEQ), dtype=rB[0]["yT"].dtype)
    for c in cores:
        yT = rB[c]["yT"]
        mixT_full[:, 2 * c, :] = yT[0]
        mixT_full[:, 2 * c + 1, :] = yT[1]
        ybT = rB[c]["ybT"]
        mixT_full[:, 16 + 2 * c, :] = ybT[0]
        mixT_full[:, 16 + 2 * c + 1, :] = ybT[1]
    del Bp, ims, rB, xnT_all
    C = build_C()
    shared = {"w_out": f32(inp["w_out"][0]), "g_mpost": f32(inp["mix_post_g"]), "g_pre": f32(inp["ffn2_pre_g"]),
              "g_post": f32(inp["ffn2_post_g"]), "g_fin": f32(inp["final_g"]),
              "wg": f32(inp["ffn2_w_gate"][0]), "wu": f32(inp["ffn2_w_up"][0]), "wd": f32(inp["ffn2_w_down"][0])}
    ims = [dict(shared, h1=h1[c], mixT=np.ascontiguousarray(mixT_full[:, :, c * NT:(c + 1) * NT])) for c in cores]
    rC = run_bass_kernel_spmd(C.nc, ims, core_ids=cores).results
    out = np.concatenate([rC[c]["out"] for c in cores], axis=0)
    return out[None].astype(np.float32)
```

```python
import numpy as np
from contextlib import ExitStack
import concourse.bass as bass
import concourse.mybir as mybir
from concourse.bass_utils import run_bass_kernel_spmd

F32 = mybir.dt.float32
BF16 = mybir.dt.bfloat16
AF = mybir.ActivationFunctionType
ALU = mybir.AluOpType
AX = mybir.AxisListType

NCORES = 8
D = 4096
SEQ = 8192
NT = SEQ // NCORES
DFF = 11008
KC = D // 128
FC = DFF // 128
TT = NT // 128
NORM_EPS = 1e-6


CLEAR_SEMS = False


class Sched:
    def __init__(self, nc, stack, n_dma_sems=56):
        self.nc = nc
        self.engs = {"pe": nc.tensor, "act": nc.scalar, "dve": nc.vector,
                     "pool": nc.gpsimd, "sp": nc.sync}
        self.ops = []
        self.dma_sems = [stack.enter_context(nc.semaphore("dq%d" % k)) for k in range(n_dma_sems)]
        self.eng_sems = {e: stack.enter_context(nc.semaphore("es_" + e)) for e in self.engs}
        self.sem_running = [0] * n_dma_sems
        self.cnt = {e: 0 for e in self.engs}
        self.total = 0

    ktag = None
    KTAG_NAMES = frozenset(("sigw", "aic", "kd", "bd", "gsb", "kk", "kk2", "rn", "kkn", "tmpa", "ksum", "rk", "bon",
                            "sigT", "xTt", "gi", "ge", "gv", "gs", "ahat", "rhat", "bhat", "khat", "btil", "ktil",
                            "yf", "ybk", "ysb", "dd", "d2", "rs", "bn", "gt", "rob"))

    def _tag(self, k):
        base = k[0] if isinstance(k, tuple) else k
        if base in self.KTAG_NAMES:
            return ("c%d" % self.ktag, k)
        return k

    def add(self, eng, fn, reads=(), writes=(), dma=None):
        if self.ktag is not None:
            reads = [self._tag(k) for k in reads]
            writes = [self._tag(k) for k in writes]
        self.ops.append((eng, fn, tuple(reads), tuple(writes), dma))

    def emit(self):
        nc = self.nc
        ops = self.ops
        n = len(ops)
        nsem = len(self.dma_sems)
        last_w = {}
        rd_eng = {}
        rd_dma = {}
        deps = [None] * n
        raw = [None] * n
        for i, (eng, fn, R, W, dma) in enumerate(ops):
            d = set()
            rw = set()
            for r in R:
                j = last_w.get(r)
                if j is not None:
                    d.add(j)
                    rw.add(j)
            raw[i] = rw
            for w in W:
                j = last_w.get(w)
                if j is not None:
                    d.add(j)
                for j in rd_eng.get(w, {}).values():
                    d.add(j)
                for j in rd_dma.get(w, ()):
                    d.add(j)
            d.discard(i)
            deps[i] = d
            for r in R:
                if dma is not None:
                    rd_dma.setdefault(r, []).append(i)
                else:
                    rd_eng.setdefault(r, {})[eng] = i
            for w in W:
                last_w[w] = i
                rd_eng[w] = {}
                rd_dma[w] = []
        key_sem = {}
        dma_val = [0] * n
        for i, (eng, fn, R, W, dma) in enumerate(ops):
            if dma is not None:
                if dma not in key_sem:
                    key_sem[dma] = len(key_sem) % nsem
                s = key_sem[dma]
                self.sem_running[s] += 16
                dma_val[i] = self.sem_running[s]
        waited = {e: {f: -1 for f in self.engs} for e in self.engs}
        waited_dma = {e: {} for e in self.engs}
        waits = [None] * n
        signal = [False] * n
        for i, (eng, fn, R, W, dma) in enumerate(ops):
            wl = []
            cw = {}
            dw = {}
            for j in deps[i]:
                ej, _, _, _, dj = ops[j]
                if dj is not None:
                    s = key_sem[dj]
                    v = dma_val[j]
                    if v > waited_dma[eng].get(s, 0):
                        dw[s] = max(dw.get(s, 0), v)
                else:
                    if ej == eng and eng == "pe":
                        continue
                    if j > waited[eng][ej]:
                        cw[ej] = max(cw.get(ej, -1), j)
            for ej, j in cw.items():
                waited[eng][ej] = j
                signal[j] = True
                wl.append(("c", ej, j))
            for s, v in dw.items():
                waited_dma[eng][s] = v
                wl.append(("d", s, v))
            waits[i] = wl
        tok = [0] * n
        for i, (eng, fn, R, W, dma) in enumerate(ops):
            if dma is None and signal[i]:
                self.cnt[eng] += 1
                tok[i] = self.cnt[eng]
        for i, (eng, fn, R, W, dma) in enumerate(ops):
            e = self.engs[eng]
            for w in waits[i]:
                if w[0] == "c":
                    e.wait_ge(self.eng_sems[w[1]], tok[w[2]])
                else:
                    e.wait_ge(self.dma_sems[w[1]], w[2])
            ins = fn(e)
            if dma is not None:
                ins.then_inc(self.dma_sems[key_sem[dma]], 16)
            elif signal[i]:
                ins.then_inc(self.eng_sems[eng], 1)
        for s in range(nsem):
            if self.sem_running[s] > 0:
                nc.sync.wait_ge(self.dma_sems[s], self.sem_running[s])
        nc.all_engine_barrier()
        if not CLEAR_SEMS:
            self.ops = []
            self.total += n
            return n
        for s in range(nsem):
            if self.sem_running[s] > 0:
                nc.sync.sem_clear(self.dma_sems[s])
                self.sem_running[s] = 0
        for e in self.engs:
            if self.cnt[e] > 0:
                nc.sync.sem_clear(self.eng_sems[e])
                self.cnt[e] = 0
        nc.all_engine_barrier()
        self.ops = []
        self.total += n
        return n


class Builder:
    def __init__(self):
        self.nc = bass.Bass("TRN2", target_bir_lowering=False)
        self.stack = ExitStack()
        self.S = Sched(self.nc, self.stack)
        self.rr = {}

    def rot(self, name, n):
        k = self.rr.get(name, 0)
        self.rr[name] = k + 1
        return k % n

    def sb(self, name, shape, dt, stack=None):
        return (stack or self.stack).enter_context(self.nc.sbuf_tensor(name, list(shape), dt))

    def dram(self, name, shape, dt, kind="Internal"):
        return self.nc.dram_tensor(name, list(shape), dt, kind=kind).ap()

    def dma(self, q, out, in_, key, reads=(), writes=(), **kw):
        self.S.add(q, lambda e, o=out, i=in_, kw=kw: e.dma_start(out=o, in_=i, **kw),
                   reads=reads, writes=writes, dma=key)

    def evac_copy(self, out, in_, reads, writes):
        if self.rot("evac", 2) == 0:
            self.S.add("act", lambda e, o=out, i=in_: e.copy(out=o, in_=i), reads, writes)
        else:
            self.S.add("dve", lambda e, o=out, i=in_: e.tensor_copy(out=o, in_=i), reads, writes)

    def setup_common(self):
        nc = self.nc
        self.ps_all = self.stack.enter_context(nc.psum_tensor("ps_all", [128, 4096], F32))
        self.ps = [self.ps_all[:, i * 512:(i + 1) * 512] for i in range(8)]
        self.ident_f = self.sb("ident_f", [128, 128], F32)
        self.ident_b = self.sb("ident_b", [128, 128], BF16)
        S = self.S
        idf, idb = self.ident_f, self.ident_b
        S.add("pool", lambda e: e.memset(idf[:], 0.0), (), ("ident_f",))
        S.add("pool", lambda e: e.affine_select(
            out=idf[:], in_=idf[:], pattern=[[-1, 128]], compare_op=ALU.not_equal,
            fill=1.0, base=0, channel_multiplier=1), ("ident_f",), ("ident_f",))
        S.add("pool", lambda e: e.tensor_copy(out=idb[:], in_=idf[:]), ("ident_f",), ("ident_b",))

    def alloc_token_phase(self, st):
        self.xT = self.sb("xT", [128, KC, NT], BF16, st)
        self.rt = [self.sb("rt%d" % i, [128, D], F32, st) for i in range(2)]
        self.xs = self.sb("xs", [128, D], BF16, st)
        self.gbc = [self.sb("gbc%d" % i, [128, D], F32, st) for i in range(2)]
        self.ss = self.sb("ss", [128, 8], F32, st)
        self.W16 = [self.sb("W16_%d" % i, [128, 1024], BF16, st) for i in range(4)]
        self.X = [self.sb("X%d" % i, [128, 1024], F32, st) for i in range(6)]

    def load_gbc(self, k, vec_ap):
        self.dma("sp", self.gbc[k][:], vec_ap.partition_broadcast(128), ("gbc", k), (), (("gbc", k),))

    def rstd_of(self, tile, tile_key, col):
        S, ss, xs = self.S, self.ss, self.xs
        S.add("act", lambda e: e.activation(out=xs[:], in_=tile[:], func=AF.Square,
                                            accum_out=ss[:, col:col + 1]),
              (tile_key,), ("xs", ("ss", col)))
        S.add("act", lambda e: e.activation(out=ss[:, col:col + 1], in_=ss[:, col:col + 1],
                                            func=AF.Sqrt, bias=NORM_EPS, scale=1.0 / D),
              (("ss", col),), (("ss", col),))
        S.add("dve", lambda e: e.reciprocal(out=ss[:, col:col + 1], in_=ss[:, col:col + 1]),
              (("ss", col),), (("ss", col),))

    def rows_to_xT(self, tile, tile_key, gk, tt, col):
        S, xs, ss, xT, idb = self.S, self.xs, self.ss, self.xT, self.ident_b
        gb = self.gbc[gk]
        S.add("dve", lambda e: e.scalar_tensor_tensor(out=xs[:], in0=tile[:], scalar=ss[:, col:col + 1],
                                                      in1=gb[:], op0=ALU.mult, op1=ALU.mult),
              (tile_key, ("ss", col), ("gbc", gk)), ("xs",))
        for q in range(KC // 4):
            b = self.rot("tpb", 2)
            psb = self.ps[b][:].bitcast(BF16)
            for j in range(4):
                dc = q * 4 + j
                S.add("pe", lambda e, j=j, dc=dc, psb=psb: e.transpose(
                    out=psb[:, j * 128:(j + 1) * 128], in_=xs[:, dc * 128:(dc + 1) * 128],
                    identity=idb[:]), ("xs", "ident_b"), (("ps", b),))
            dst = xT[:, q * 4:(q + 1) * 4, tt * 128:(tt + 1) * 128]
            src = psb[:, 0:512].rearrange("p (j t) -> p j t", j=4)
            self.evac_copy(dst, src, (("ps", b),), (("xT", tt, q),))

    def norm_phase(self, src, g_vec, src_key="x"):
        self.load_gbc(0, g_vec)
        for tt in range(TT):
            k = self.rot("rt", 2)
            self.dma("sp", self.rt[k][:], src[tt * 128:(tt + 1) * 128, :], ("rt", k),
                     ((src_key, tt),), (("rt", k),))
            self.rstd_of(self.rt[k], ("rt", k), tt)
            self.rows_to_xT(self.rt[k], ("rt", k), 0, tt, tt)

    def xT_keys(self, dc, half):
        return [("xT", tt, dc // 4) for tt in range(half * 4, half * 4 + 4)]

    def ffn_up(self, wg, wu, hT_dram):
        S = self.S
        for fg in range(FC // 2):
            sgk = self.rot("sg", 2)
            sg = [self.X[sgk * 2], self.X[sgk * 2 + 1]]
            sgkeys = [("X", sgk * 2), ("X", sgk * 2 + 1)]
            hk = 4 + self.rot("hst", 2)
            hst = self.X[hk][:].bitcast(BF16).rearrange("p (c t) -> p c t", c=2)
            for mi, W in enumerate((wg, wu)):
                for dq in range(KC // 4):
                    k = self.rot("W16", 4)
                    wbf = self.W16[k][:].rearrange("p (c f) -> p c f", c=4)
                    src = W[dq * 512:(dq + 1) * 512, fg * 256:(fg + 1) * 256].rearrange(
                        "(c p) f -> p c f", p=128)
                    self.dma("pool", wbf, src, ("W16", k), (), (("W16", k),))
                    for c in range(4):
                        dc = dq * 4 + c
                        for fc in range(2):
                            for half in range(2):
                                b = mi * 4 + fc * 2 + half
                                S.add("pe", lambda e, b=b, c=c, fc=fc, half=half, dc=dc, wbf=wbf: e.matmul(
                                    self.ps[b][:], lhsT=wbf[:, c, fc * 128:(fc + 1) * 128],
                                    rhs=self.xT[:, dc, half * 512:(half + 1) * 512],
                                    start=(dc == 0), stop=(dc == KC - 1)),
                                    [("W16", k)] + self.xT_keys(dc, half), (("ps", b),))
                if mi == 0:
                    for fc in range(2):
                        for half in range(2):
                            b = fc * 2 + half
                            S.add("act", lambda e, b=b, fc=fc, half=half, sg=sg: e.activation(
                                out=sg[fc][:, half * 512:(half + 1) * 512], in_=self.ps[b][:], func=AF.Silu),
                                (("ps", b),), (sgkeys[fc],))
                else:
                    for fc in range(2):
                        for half in range(2):
                            b = 4 + fc * 2 + half
                            S.add("dve", lambda e, b=b, fc=fc, half=half, sg=sg, hst=hst: e.tensor_tensor(
                                out=hst[:, fc, half * 512:(half + 1) * 512], in0=self.ps[b][:],
                                in1=sg[fc][:, half * 512:(half + 1) * 512], op=ALU.mult),
                                (("ps", b), sgkeys[fc]), (("X", hk),))
            self.dma("act", hT_dram[fg * 2:(fg + 1) * 2].rearrange("c p t -> p c t"), hst,
                     ("hst", hk), (("X", hk),), (("hT", fg),))

    def ffn_down(self, hT_dram, wd, f1_dram, f_key="f1"):
        S = self.S
        NQ = FC // 2
        for dt in range(D // 512):
            for fq in range(NQ):
                k = self.rot("W16", 4)
                wbf = self.W16[k][:].rearrange("p (c n) -> p c n", c=2)
                src = wd[fq * 256:(fq + 1) * 256, dt * 512:(dt + 1) * 512].rearrange("(c p) n -> p c n", p=128)
                self.dma("pool", wbf, src, ("W16", k), (), (("W16", k),))
                hk = self.rot("hin", 4)
                hin = self.X[hk][:].bitcast(BF16).rearrange("p (c t) -> p c t", c=2)
                self.dma("sp", hin, hT_dram[fq * 2:(fq + 1) * 2].rearrange("c p t -> p c t"),
                         ("hin", hk), (("hT", fq),), (("X", hk),))
                for c in range(2):
                    for tt in range(TT):
                        S.add("pe", lambda e, c=c, tt=tt, hin=hin, wbf=wbf, fq=fq: e.matmul(
                            self.ps[tt][:], lhsT=hin[:, c, tt * 128:(tt + 1) * 128], rhs=wbf[:, c, :],
                            start=(fq == 0 and c == 0), stop=(fq == NQ - 1 and c == 1)),
                            (("X", hk), ("W16", k)), (("ps", tt),))
            for tt in range(TT):
                ok = 4 + self.rot("ost", 2)
                ost = self.X[ok][:, 0:512]
                self.evac_copy(ost, self.ps[tt][:], (("ps", tt),), (("X", ok),))
                self.dma("act", f1_dram[tt * 128:(tt + 1) * 128, dt * 512:(dt + 1) * 512], ost,
                         ("ost", ok), (("X", ok),), ((f_key, tt, dt),))

    def post_phase(self, f_dram, h_src, g_post, scale, h_dst, next_g, final_out=None,
                   f_key="f1", hs_key="x", hd_key="h"):
        S = self.S
        self.load_gbc(0, g_post)
        self.load_gbc(1, next_g)
        r0, r1 = self.rt
        for tt in range(TT):
            rows = slice(tt * 128, (tt + 1) * 128)
            self.dma("sp", r0[:], f_dram[rows, :], ("rt", 0),
                     [(f_key, tt, dt) for dt in range(D // 512)], (("rt", 0),))
            self.dma("sp", r1[:], h_src[rows, :], ("rt", 1), ((hs_key, tt),), (("rt", 1),))
            self.rstd_of(r0, ("rt", 0), 0)
            S.add("dve", lambda e: e.scalar_tensor_tensor(out=r0[:], in0=r0[:], scalar=self.ss[:, 0:1],
                                                          in1=self.gbc[0][:], op0=ALU.mult, op1=ALU.mult),
                  (("rt", 0), ("ss", 0), ("gbc", 0)), (("rt", 0),))
            S.add("dve", lambda e: e.scalar_tensor_tensor(out=r1[:], in0=r0[:], scalar=float(scale),
                                                          in1=r1[:], op0=ALU.mult, op1=ALU.add),
                  (("rt", 0), ("rt", 1)), (("rt", 1),))
            if h_dst is not None:
                self.dma("act", h_dst[rows, :], r1[:], ("hdst",), (("rt", 1),), ((hd_key, tt),))
            self.rstd_of(r1, ("rt", 1), 1)
            if final_out is None:
                self.rows_to_xT(r1, ("rt", 1), 1, tt, 1)
            else:
                S.add("dve", lambda e: e.scalar_tensor_tensor(out=r0[:], in0=r1[:], scalar=self.ss[:, 1:2],
                                                              in1=self.gbc[1][:], op0=ALU.mult, op1=ALU.mult),
                      (("rt", 1), ("ss", 1), ("gbc", 1)), (("rt", 0),))
                self.dma("act", final_out[rows, :], r0[:], ("fout",), (("rt", 0),), ())

    def linear_tok(self, W, col_tiles, sink):
        S = self.S
        for ci, (c0, wd_) in enumerate(col_tiles):
            for dq in range(KC // 2):
                k = self.rot("W16", 4)
                wbf = self.W16[k][:].rearrange("p (c n) -> p c n", c=2)
                src = W[dq * 256:(dq + 1) * 256, c0:c0 + wd_].rearrange("(c p) n -> p c n", p=128)
                self.dma("pool", wbf[:, :, 0:wd_], src, ("W16", k), (), (("W16", k),))
                for c in range(2):
                    dc = dq * 2 + c
                    for tt in range(TT):
                        S.add("pe", lambda e, c=c, tt=tt, dc=dc, wbf=wbf, wd_=wd_: e.matmul(
                            self.ps[tt][:, 0:wd_], lhsT=self.xT[:, dc, tt * 128:(tt + 1) * 128],
                            rhs=wbf[:, c, 0:wd_], start=(dc == 0), stop=(dc == KC - 1)),
                            (("W16", k), ("xT", tt, dc // 4)), (("ps", tt),))
            for tt in range(TT):
                ok = 4 + self.rot("ost", 2)
                ost = self.X[ok][:, 0:wd_]
                self.evac_copy(ost, self.ps[tt][:, 0:wd_], (("ps", tt),), (("X", ok),))
                sink(tt, ci, ost, ok)


def build_A():
    B = Builder()
    nc = B.nc
    I = lambda n, s: nc.dram_tensor(n, list(s), F32, kind="ExternalInput").ap()
    x = I("x", [NT, D])
    g_pre, g_post, g_next = I("g_pre", [1, D]), I("g_post", [1, D]), I("g_next", [1, D])
    wg, wu, wd = I("wg", [D, DFF]), I("wu", [D, DFF]), I("wd", [DFF, D])
    h1 = nc.dram_tensor("h1", [NT, D], F32, kind="ExternalOutput").ap()
    xn2T = nc.dram_tensor("xn2T", [128, KC, NT], BF16, kind="ExternalOutput").ap()
    hT = B.dram("hT", [FC, 128, NT], BF16)
    f1 = B.dram("f1", [NT, D], F32)
    B.setup_common()
    B.alloc_token_phase(B.stack)
    B.norm_phase(x, g_pre[0])
    B.ffn_up(wg, wu, hT)
    B.ffn_down(hT, wd, f1)
    B.post_phase(f1, x, g_post[0], 0.5, h1, g_next[0])
    B.dma("sp", xn2T, B.xT[:], "xo", [("xT", tt, q) for tt in range(TT) for q in range(KC // 4)], ())
    B.S.emit()
    return B


def build_C():
    B = Builder()
    nc = B.nc
    I = lambda n, s: nc.dram_tensor(n, list(s), F32, kind="ExternalInput").ap()
    h1 = I("h1", [NT, D])
    mixT = nc.dram_tensor("mixT", [128, KC, NT], BF16, kind="ExternalInput").ap()
    w_out = I("w_out", [D, D])
    g_mpost, g_pre, g_post, g_fin = I("g_mpost", [1, D]), I("g_pre", [1, D]), I("g_post", [1, D]), I("g_fin", [1, D])
    wg, wu, wd = I("wg", [D, DFF]), I("wu", [D, DFF]), I("wd", [DFF, D])
    out = nc.dram_tensor("out", [NT, D], F32, kind="ExternalOutput").ap()
    hT = B.dram("hT", [FC, 128, NT], BF16)
    f2 = B.dram("f2", [NT, D], F32)
    mx = B.dram("mx", [NT, D], F32)
    h2 = B.dram("h2", [NT, D], F32)
    B.setup_common()
    B.alloc_token_phase(B.stack)
    B.dma("sp", B.xT[:], mixT, "xi", (), [("xT", tt, q) for tt in range(TT) for q in range(KC // 4)])

    def sink(tt, ci, ost, ok):
        B.dma("act", mx[tt * 128:(tt + 1) * 128, ci * 512:(ci + 1) * 512], ost, ("ost", ok),
              (("X", ok),), (("mx", tt, ci),))
    B.linear_tok(w_out, [(c * 512, 512) for c in range(D // 512)], sink)
    B.post_phase(mx, h1, g_mpost[0], 1.0, h2, g_pre[0], f_key="mx", hs_key="h1", hd_key="h2")
    B.ffn_up(wg, wu, hT)
    B.ffn_down(hT, wd, f2, f_key="f2")
    B.post_phase(f2, h2, g_post[0], 0.5, None, g_fin[0], final_out=out, f_key="f2", hs_key="h2")
    B.S.emit()
    return B


I32 = mybir.dt.int32
NBLK = SEQ // NT
RC_OFF = [0, 128, 256, 384, 512, 640, 768, 864, 960, 1088, 1216, 1344, 1472, 1600]
RC_ROWS = [128] * 6 + [96, 96] + [128] * 6
DV_OFF = 1728
WB_COLS = 1984
LN2 = 0.6931471805599453
SUBLN_EPS = 1e-5
LAM_INIT = 0.2


class BuilderB(Builder):
    def proj_phase(self, xnT_all, Wb, PR, st):
        S = self.S
        xT = self.sb("xTB", [128, KC, NT], BF16, st)
        wr = [self.sb("wr%d" % i, [128, KC, 128], BF16, st) for i in range(2)]
        stg = [self.sb("stg%d" % i, [128, 512], F32, st) for i in range(3)]

        zt = stg[0]
        S.add("dve", lambda e: e.memset(zt[:, 0:1], 0.0), (), (("stg", 0),))
        for rc in range(10):
            self.dma("sp", PR[rc][:, 0:1], zt[:, 0:1], "prz", (("stg", 0),), (("PRz", rc),),
                     allow_slow_non_contiguous=True)
            self.dma("sp", PR[rc][:, SEQ + 1:SEQ + 2], zt[:, 0:1], "prz", (("stg", 0),), (("PRz", rc),),
                     allow_slow_non_contiguous=True)
        for blk in range(NBLK):
            self.dma("sp", xT[:], xnT_all[blk], "xTB", (), ("xTB",))
            for rc in range(14):
                rows = RC_ROWS[rc]
                k = self.rot("wr", 2)
                src = Wb[:, RC_OFF[rc]:RC_OFF[rc] + rows].rearrange("(dc p) n -> p dc n", p=128)
                self.dma("pool", wr[k][:, :, 0:rows], src, ("wr", k), (), (("wr", k),))
                for half in range(2):
                    b = self.rot("pb1", 4)
                    for dc in range(KC):
                        S.add("pe", lambda e, b=b, dc=dc, k=k, rows=rows, half=half: e.matmul(
                            self.ps[b][0:rows, :], lhsT=wr[k][:, dc, 0:rows],
                            rhs=xT[:, dc, half * 512:(half + 1) * 512],
                            start=(dc == 0), stop=(dc == KC - 1)), (("wr", k), "xTB"), (("ps", b),))
                    t0 = blk * NT + half * 512
                    if rc < 10:
                        sk = self.rot("stg", 3)
                        self.evac_copy(stg[sk][0:rows, :], self.ps[b][0:rows, :], (("ps", b),), (("stg", sk),))
                        self.dma("act", PR[rc][0:rows, 1 + t0:1 + t0 + 512], stg[sk][0:rows, :], ("stgo", sk),
                                 (("stg", sk),), (("PR", rc, blk),))
                    elif rc < 12:
                        h = rc - 10
                        S.add("act", lambda e, b=b, h=h, t0=t0: e.activation(
                            out=self.qT[h][:, t0:t0 + 512], in_=self.ps[b][:], func=AF.Identity,
                            scale=self.qs[:, h:h + 1]), (("ps", b), "qs"), (("qT", h),))
                    else:
                        h = rc - 12
                        self.evac_copy(self.kT[h][:, t0:t0 + 512], self.ps[b][:], (("ps", b),), (("kT", h),))
            for h in range(2):
                k = self.rot("wr", 2)
                src = Wb[:, DV_OFF + h * 128:DV_OFF + (h + 1) * 128].rearrange("(dc p) n -> p dc n", p=128)
                self.dma("pool", wr[k][:], src, ("wr", k), (), (("wr", k),))
                for tt in range(TT):
                    b = self.rot("pb1", 4)
                    for dc in range(KC):
                        S.add("pe", lambda e, b=b, dc=dc, k=k, tt=tt: e.matmul(
                            self.ps[b][:, 0:128], lhsT=xT[:, dc, tt * 128:(tt + 1) * 128],
                            rhs=wr[k][:, dc, :], start=(dc == 0), stop=(dc == KC - 1)),
                            (("wr", k), "xTB"), (("ps", b),))
                    self.evac_copy(self.Vx[h][:, blk * TT + tt, 0:128], self.ps[b][:, 0:128],
                                   (("ps", b),), (("Vx", h),))

    def attn_consts(self, hidx, lq1, lk1, lq2, lk2, subln_g, st):
        S = self.S
        self.qs = self.sb("qs", [128, 2], F32, st)
        self.slope = self.sb("slope", [128, 2], F32, st)
        self.negc = self.sb("negc", [128, 2], F32, st)
        self.lam = self.sb("lam", [128, 4], F32, st)
        self.gsub = self.sb("gsub", [128, 128], F32, st)
        hid = self.sb("hid", [128, 2], F32, st)
        lt = self.sb("lt", [128, 4, 64], F32, st)
        self.dma("sp", hid[:], hidx, "c0", (), ("hid",))
        for i, v in enumerate((lq1, lk1, lq2, lk2)):
            self.dma("sp", lt[:, i, :], v.partition_broadcast(128), "c1", (), ("lt",))
        self.dma("sp", self.gsub[:], subln_g.partition_broadcast(128), "c2", (), ("gsub",))
        self.gsubc = self.sb("gsubc", [128, 1], F32, st)
        self.dma("sp", self.gsubc[:], subln_g.rearrange("(v o) -> v o", o=1), "c3", (), ("gsubc",))
        S.add("dve", lambda e: e.tensor_scalar(out=self.gsubc[:], in0=self.gsubc[:], scalar1=1.0 - LAM_INIT,
                                               scalar2=None, op0=ALU.mult), ("gsubc",), ("gsubc",))
        S.add("act", lambda e: e.activation(out=self.slope[:], in_=hid[:], func=AF.Exp, scale=-0.5 * LN2),
              ("hid",), ("slope",))
        S.add("act", lambda e: e.activation(out=self.qs[:], in_=hid[:], func=AF.Exp, scale=0.5 * LN2),
              ("hid",), ("qs",))
        S.add("dve", lambda e: e.tensor_scalar(out=self.qs[:], in0=self.qs[:], scalar1=0.125, scalar2=None,
                                               op0=ALU.mult), ("qs",), ("qs",))
        S.add("dve", lambda e: e.memset(self.negc[:], 0.0), (), ("negc",))
        S.add("dve", lambda e: e.tensor_tensor(out=lt[:, 0, :], in0=lt[:, 0, :], in1=lt[:, 1, :], op=ALU.mult),
              ("lt",), ("lt",))
        S.add("dve", lambda e: e.tensor_tensor(out=lt[:, 2, :], in0=lt[:, 2, :], in1=lt[:, 3, :], op=ALU.mult),
              ("lt",), ("lt",))
        S.add("dve", lambda e: e.reduce_sum(out=self.lam[:, 0:1], in_=lt[:, 0, :], axis=AX.X), ("lt",), ("lam",))
        S.add("dve", lambda e: e.reduce_sum(out=self.lam[:, 1:2], in_=lt[:, 2, :], axis=AX.X), ("lam", "lt"), ("lam",))
        S.add("act", lambda e: e.activation(out=self.lam[:, 0:2], in_=self.lam[:, 0:2], func=AF.Exp),
              ("lam",), ("lam",))
        S.add("dve", lambda e: e.tensor_tensor(out=self.lam[:, 2:3], in0=self.lam[:, 0:1], in1=self.lam[:, 1:2],
                                               op=ALU.subtract), ("lam",), ("lam",))
        S.add("dve", lambda e: e.tensor_scalar(out=self.lam[:, 3:4], in0=self.lam[:, 2:3], scalar1=LAM_INIT,
                                               scalar2=-1.0, op0=ALU.add, op1=ALU.mult), ("lam",), ("lam",))
        S.add("dve", lambda e: e.tensor_scalar(out=self.gsub[:], in0=self.gsub[:], scalar1=1.0 - LAM_INIT,
                                               scalar2=None, op0=ALU.mult), ("gsub",), ("gsub",))

    def attn_phase(self, ybT, st, dbg=None):
        S = self.S
        NW = 2 * SEQ
        NKT = SEQ // 128
        LOOK = 3
        T0 = self.sb("T0", [128, NW], mybir.dt.int16, st)
        it = self.sb("iota", [128, 2048], I32, st)
        itf = self.sb("iotaf", [128, 2048], F32, st)
        tmp = [self.sb("atmp%d" % i, [128, 512], F32, st) for i in range(4)]
        PT = [self.sb("PT%d" % i, [128, 512], BF16, st) for i in range(4)]
        onb = self.sb("onb", [128, 128], BF16, st)
        onf = self.sb("onf", [128, 128], F32, st)
        rz = [self.sb("rz%d" % i, [128, 512], F32, st) for i in range(2)]
        o_ = self.sb("ao", [128, 512], F32, st)
        sq = self.sb("asq", [128, 512], F32, st)
        obf = [self.sb("aob%d" % i, [128, 512], BF16, st) for i in range(2)]
        S.add("dve", lambda e: e.memset(onb[:], 1.0), (), ("onb",))
        S.add("dve", lambda e: e.memset(onf[:], 1.0), (), ("onf",))
        for c in range(NW // 2048):
            S.add("pool", lambda e, c=c: e.iota(it[:], pattern=[[1, 2048]], base=c * 2048 - SEQ,
                                                 channel_multiplier=-1), (), ("iota",))
            S.add("dve", lambda e, c=c: e.tensor_copy(out=itf[:], in_=it[:]), ("iota",), ("iotaf",))
            S.add("dve", lambda e, c=c: e.scalar_tensor_tensor(
                out=T0[:, c * 2048:(c + 1) * 2048], in0=itf[:], scalar=-1.0,
                in1=itf[:], op0=ALU.mult, op1=ALU.min), ("iotaf",), ("T0",))
        kz = [self.sb("kz%d" % m, [128, SEQ], BF16, st) for m in range(2)]
        for m in range(2):
            z = kz[m][64 * (1 - m):64 * (1 - m) + 64, :]
            S.add("pool", lambda e, z=z: e.memset(z, 0.0), (), (("kz", m),))
        for h in range(2 if dbg is None else dbg[0]):
            for m in range(2):
                S.add("pool", lambda e, h=h, m=m: e.tensor_copy(out=kz[m][64 * m:64 * m + 64, :],
                                                                in_=self.kT[h][64 * m:64 * m + 64, :]),
                      (("kT", h),), (("kz", m),))
            for g in range(SEQ // 512 if dbg is None else dbg[1]):
                def scores(u, h=h, g=g):
                    kt, m = u // 2, u % 2
                    b = u % 4
                    S.add("pe", lambda e, b=b, m=m, kt=kt: e.matmul(
                        self.ps[b][:], lhsT=kz[m][:, kt * 128:(kt + 1) * 128],
                        rhs=self.qT[h][:, g * 512:(g + 1) * 512], start=True, stop=True),
                        (("kz", m), ("qT", h)), (("ps", b),))
                for u in range(LOOK):
                    scores(u)
                for u in range(2 * NKT):
                    if u + LOOK < 2 * NKT:
                        scores(u + LOOK)
                    kt, m = u // 2, u % 2
                    b = u % 4
                    r = u % 4
                    off = SEQ + 512 * g - 128 * kt
                    S.add("dve", lambda e, b=b, r=r, off=off: e.tensor_tensor(
                        out=tmp[r][:], in0=self.ps[b][:], in1=T0[:, off:off + 512], op=ALU.add),
                        (("ps", b), "T0"), (("atmp", r),))
                    S.add("act", lambda e, r=r, h=h: e.activation(
                        out=PT[r][:], in_=tmp[r][:], func=AF.Exp, scale=self.slope[:, h:h + 1],
                        bias=self.negc[:, h:h + 1]), (("atmp", r), "slope", "negc"), (("PT", r),))
                    S.add("pe", lambda e, m=m, r=r, h=h, kt=kt: e.matmul(
                        self.ps[4 + m][:], lhsT=self.Vx[h][:, kt, 0:128], rhs=PT[r][:],
                        start=(kt == 0), stop=(kt == NKT - 1)), (("PT", r), ("Vx", h)), (("ps", 4 + m),))
                    S.add("pe", lambda e, m=m, r=r, kt=kt: e.matmul(
                        self.ps[6 + m][:], lhsT=onb[:], rhs=PT[r][:],
                        start=(kt == 0), stop=(kt == NKT - 1)), (("PT", r), "onb"), (("ps", 6 + m),))
                for m in range(2):
                    S.add("dve", lambda e, m=m: e.reciprocal(out=rz[m][:], in_=self.ps[6 + m][:]),
                          (("ps", 6 + m),), (("rz", m),))
                S.add("dve", lambda e: e.tensor_tensor(out=o_[:], in0=self.ps[4][:], in1=rz[0][:], op=ALU.mult),
                      (("ps", 4), ("rz", 0)), ("ao",))
                S.add("dve", lambda e: e.tensor_tensor(out=rz[1][:], in0=self.ps[5][:], in1=rz[1][:], op=ALU.mult),
                      (("ps", 5), ("rz", 1)), (("rz", 1),))
                S.add("dve", lambda e: e.scalar_tensor_tensor(out=o_[:], in0=rz[1][:], scalar=self.lam[:, 3:4],
                                                              in1=o_[:], op0=ALU.mult, op1=ALU.add),
                      (("rz", 1), "lam", "ao"), ("ao",))
                S.add("act", lambda e: e.activation(out=sq[:], in_=o_[:], func=AF.Square), ("ao",), ("asq",))
                bq = (2 * NKT + LOOK) % 4
                S.add("pe", lambda e, bq=bq: e.matmul(self.ps[bq][:], lhsT=onf[:], rhs=sq[:], start=True, stop=True),
                      ("onf", "asq"), (("ps", bq),))
                S.add("act", lambda e, bq=bq: e.activation(out=sq[:], in_=self.ps[bq][:], func=AF.Sqrt,
                                                           bias=SUBLN_EPS, scale=1.0 / 128),
                      (("ps", bq),), ("asq",))
                S.add("dve", lambda e: e.reciprocal(out=sq[:], in_=sq[:]), ("asq",), ("asq",))
                ok = self.rot("aob", 2)
                S.add("dve", lambda e, ok=ok: e.scalar_tensor_tensor(
                    out=obf[ok][:], in0=o_[:], scalar=self.gsubc[:, 0:1], in1=sq[:], op0=ALU.mult, op1=ALU.mult),
                    ("ao", "asq", "gsubc"), (("aob", ok),))
                self.dma("sp", ybT[h][:, g * 512:(g + 1) * 512], obf[ok][:], ("obo", ok), (("aob", ok),), ())
                if g % 8 == 7:
                    S.emit()


def build_B(with_rwkv=True, with_attn=True, dbg=None, rdbg=None):
    B = BuilderB()
    nc = B.nc
    I = lambda n, s: nc.dram_tensor(n, list(s), F32, kind="ExternalInput").ap()
    xnT_all = nc.dram_tensor("xnT_all", [NBLK, 128, KC, NT], BF16, kind="ExternalInput").ap()
    Wb = I("Wb", [D, WB_COLS])
    hidx = I("hidx", [128, 2])
    lq1, lk1, lq2, lk2 = I("lq1", [1, 64]), I("lk1", [1, 64]), I("lq2", [1, 64]), I("lk2", [1, 64])
    subln_g = I("subln_g", [1, 128])
    yb = nc.dram_tensor("ybT", [2, 128, SEQ], BF16, kind="ExternalOutput").ap()
    PR = [B.dram("PR%d" % i, [128, SEQ + 2], F32) for i in range(10)]
    B.PR = PR
    if with_rwkv:
        B.rwkv_inputs(I)
    B.setup_common()
    st_attn = B.stack.enter_context(ExitStack())
    B.qT = [B.sb("qT%d" % h, [128, SEQ], BF16, st_attn) for h in range(2)]
    B.kT = [B.sb("kT%d" % h, [128, SEQ], BF16, st_attn) for h in range(2)]
    B.Vx = [B.sb("Vx%d" % h, [128, SEQ // 128, 128], BF16, st_attn) for h in range(2)]
    B.attn_consts(hidx, lq1[0], lk1[0], lq2[0], lk2[0], subln_g[0], st_attn)
    with ExitStack() as st:
        B.proj_phase(xnT_all, Wb, PR, st)
        B.S.emit()
    if with_attn:
        with ExitStack() as st:
            B.attn_phase(yb, st, dbg)
            B.S.emit()
    st_attn.close()
    if with_rwkv:
        B.rwkv_phase(rdbg)
    return B


def wb_for_core(w_in, c):
    R = 2048
    cols = []
    for base in (0, R, 2 * R):
        cols.append(np.arange(base + 256 * c, base + 256 * c + 256))
    cols.append(np.arange(3 * R, 3 * R + 448))
    dbase = 3 * R + 448
    for base in (0, R, 2 * R):
        cols.append(np.arange(dbase + base + 256 * c, dbase + base + 256 * c + 256))
    cols = np.concatenate(cols)
    assert cols.size == WB_COLS
    return np.ascontiguousarray(w_in[:, cols]), cols


GN_EPS = 64e-5
CDEC = -0.6065306597126334
CHK = 64
NCHK = SEQ // CHK


def _rwkv_methods():
    def o_tt(self, eng, out, in0, in1, op, R, W):
        self.S.add(eng, lambda e: e.tensor_tensor(out=out, in0=in0, in1=in1, op=op), R, W)

    def o_ts(self, eng, out, in0, s1, s2, op0, op1, R, W):
        if s2 is None:
            self.S.add(eng, lambda e: e.tensor_scalar(out=out, in0=in0, scalar1=s1, scalar2=None, op0=op0), R, W)
        else:
            self.S.add(eng, lambda e: e.tensor_scalar(out=out, in0=in0, scalar1=s1, scalar2=s2, op0=op0, op1=op1), R, W)

    def o_stt(self, eng, out, in0, sc, in1, op0, op1, R, W):
        self.S.add(eng, lambda e: e.scalar_tensor_tensor(out=out, in0=in0, scalar=sc, in1=in1, op0=op0, op1=op1), R, W)

    def o_act(self, out, in_, func, R, W, bias=None, scale=None):
        kw = {}
        if bias is not None:
            kw["bias"] = bias
        if scale is not None:
            kw["scale"] = scale
        self.S.add("act", lambda e: e.activation(out=out, in_=in_, func=func, **kw), R, W)

    def o_mm(self, out, lhsT, rhs, R, W, start=True, stop=True):
        self.S.add("pe", lambda e: e.matmul(out, lhsT=lhsT, rhs=rhs, start=start, stop=stop), R, W)

    def o_tp(self, out, in_, R, W):
        idf = self.ident_f
        n = in_.shape[0]
        self.S.add("pe", lambda e: e.transpose(out=out, in_=in_, identity=idf[0:n, 0:n]), list(R) + ["ident_f"], W)

    def o_rec(self, out, in_, R, W):
        self.S.add("dve", lambda e: e.reciprocal(out=out, in_=in_), R, W)

    def o_memset(self, eng, ap, val, W):
        self.S.add(eng, lambda e: e.memset(ap, val), (), W)

    def o_asel(self, out, in_, pattern, cmp_, cm, R, W):
        self.S.add("pool", lambda e: e.affine_select(out=out, in_=in_, pattern=pattern, compare_op=cmp_,
                                                     fill=0.0, base=0, channel_multiplier=cm), R, W)
    return dict(o_tt=o_tt, o_ts=o_ts, o_stt=o_stt, o_act=o_act, o_mm=o_mm, o_tp=o_tp, o_rec=o_rec,
                o_memset=o_memset, o_asel=o_asel)


for _k, _v in _rwkv_methods().items():
    setattr(BuilderB, _k, _v)


def _rwkv_inputs(self, I):
    nc = self.nc
    self.mu_in = I("mu", [128, 2, 10])
    self.pv_in = I("pv", [128, 9, 2])
    self.w2a2_in = I("w2a2", [96, 4, 256])
    self.g2_in = I("g2w", [128, 2, 256])
    self.yT = nc.dram_tensor("yT", [2, 128, SEQ], BF16, kind="ExternalOutput").ap()
    D_ = lambda n, s: self.dram(n, s, F32)
    self.FM = [[D_("FM%d%d" % (ch, d), [4, 128, SEQ]) for d in range(2)] for ch in range(2)]
    self.TM = [[D_("TM%d%d" % (ch, d), [3, SEQ, 128]) for d in range(2)] for ch in range(2)]
    self.VT = [D_("VT%d" % ch, [SEQ, 128]) for ch in range(2)]
    self.BN = [D_("BN%d" % ch, [128, SEQ]) for ch in range(2)]
    self.GT = [D_("GT%d" % ch, [128, SEQ]) for ch in range(2)]
    self.YO = [[D_("YO%d%d" % (ch, d), [SEQ, 128]) for d in range(2)] for ch in range(2)]


def _rwkv_consts(self, st):
    f = lambda n, s: self.sb(n, s, F32, st)
    self.BO = f("BO", [128, 128])
    self.BO64 = f("BO64", [128, 128])
    self.ones = f("ones", [128, 128])
    self.TRI = {k: f("TRI" + k, [128, 128]) for k in ("Li", "Ls", "Us", "Ui")}
    self.MT = [f("MT%d" % d, [128, 128]) for d in range(2)]
    self.MAB = [f("MAB%d" % d, [64, 64]) for d in range(2)]
    self.mu = f("mu_sb", [128, 2, 10])
    self.c0 = f("c0_sb", [128, 10])
    self.pv = f("pv_sb", [128, 9, 2])
    self.w2a2 = self.sb("w2a2_sb", [96, 4, 256], F32, st)
    self.g2 = f("g2_sb", [128, 2, 256])
    self.gc = [[f("gc%d%d" % (ch, d), [128, NCHK]) for d in range(2)] for ch in range(2)]
    self.S0T = [[f("S0T%d%d" % (ch, d), [128, 64]) for d in range(2)] for ch in range(2)]
    BO, BO64, ones = self.BO, self.BO64, self.ones
    self.o_memset("pool", ones[:], 1.0, ("ones",))
    self.o_memset("pool", BO[:], 0.0, ("BO",))
    self.o_memset("pool", BO[0:64, 0:64], 1.0, ("BO",))
    self.o_memset("pool", BO[64:128, 64:128], 1.0, ("BO",))
    self.o_ts("pool", BO64[:], BO[:], 1.0 / 64, None, ALU.mult, None, ("BO",), ("BO64",))
    for k, (pat, cm, cmp_) in {"Li": ([[1, 128]], -1, ALU.is_ge), "Ls": ([[1, 128]], -1, ALU.is_gt),
                               "Us": ([[-1, 128]], 1, ALU.is_gt), "Ui": ([[-1, 128]], 1, ALU.is_ge)}.items():
        T_ = self.TRI[k]
        self.o_ts("pool", T_[:], BO[:], CDEC, None, ALU.mult, None, ("BO",), (("TRI", k),))
        self.o_asel(T_[:], T_[:], pat, cmp_, cm, (("TRI", k),), (("TRI", k),))
    for d in range(2):
        M = self.MT[d]
        for r0 in (0, 64):
            for c0_, strict in ((0, True), (64, False)):
                if d == 0:
                    pat, cm = [[1, 64]], -1
                else:
                    pat, cm = [[-1, 64]], 1
                self.o_asel(M[r0:r0 + 64, c0_:c0_ + 64], ones[r0:r0 + 64, 0:64], pat,
                            ALU.is_gt if strict else ALU.is_ge, cm, ("ones",), (("MT", d),))
        pat, cm = ([[-1, 64]], 1) if d == 0 else ([[1, 64]], -1)
        self.o_asel(self.MAB[d][:], ones[0:64, 0:64], pat, ALU.is_gt, cm, ("ones",), (("MAB", d),))
    self.dma("sp", self.mu[:], self.mu_in, "rc0", (), ("mu",))
    self.dma("sp", self.pv[:], self.pv_in, "rc1", (), ("pv",))
    self.dma("sp", self.w2a2[:], self.w2a2_in, "rc2", (), ("w2a2",))
    self.dma("sp", self.g2[:], self.g2_in, "rc3", (), ("g2",))
    self.o_tt("dve", self.c0[:], self.mu[:, 0, :], self.mu[:, 1, :], ALU.add, ("mu",), ("c0",))
    self.o_ts("dve", self.c0[:], self.c0[:], -1.0, 1.0, ALU.mult, ALU.add, ("c0",), ("c0",))
    for ch in range(2):
        for d in range(2):
            self.o_memset("dve", self.S0T[ch][d][:], 0.0, (("S0T", ch, d),))


def _rwkv_prep(self, st, ntiles=None):
    S = self.S
    TB = 512
    f = lambda n, s=(128, TB): self.sb(n, list(s), F32, st)
    PERCH = ("sigw0", "sigw1", "aic0", "aic1", "kd0", "kd1", "bd0", "bd1", "gsb", "kk", "kk2", "rn", "kkn", "tmpa",
             "ksum", "rk", "bon", "g_incl", "g_excl", "g_inv", "g_sfx", "ahat", "rhat", "bhat", "khat", "btil", "ktil")
    sets = [{n: f("%s_c%d" % (n, c)) for n in PERCH} for c in range(2)]
    sigT2 = [f("sigT_c%d" % c, (128, 4, 128)) for c in range(2)]
    xT2 = [[f("xTt%d_c%d" % (i, c), (128, 4, 128)) for i in range(2)] for c in range(2)]
    raw = [f("raw%d" % rc, (128, TB + 2)) for rc in range(10)]
    sh = [f("sh%d" % rc) for rc in range(10)]
    hw, sg0, sg1 = f("hw"), f("sg0"), f("sg1")
    cur = {}
    seg_marks = []
    pv, mu, c0 = self.pv, self.mu, self.c0

    def bank():
        c_ = cur.get("ch", 0)
        b = 4 * c_ + self.rot("r1ps%d" % c_, 4)
        return b, self.ps[b]

    def to_tokmajor(src, src_key, dst_dram, dst_key):
        b, p = bank()
        for j in range(4):
            self.o_tp(p[:, j * 128:(j + 1) * 128], src[:, j * 128:(j + 1) * 128], (src_key,), (("ps", b),))
        k = self.rot("xTt", 2)
        xT_ = cur["xT"]
        self.evac_copy(xT_[k][:].rearrange("p j c -> p (j c)"), p, (("ps", b),), (("xTt", k),))
        self.dma("act", dst_dram.rearrange("(j p) c -> p j c", p=128), xT_[k][:], ("xTto", k, cur["ch"]),
                 (("xTt", k),), (dst_key,))

    for tb in range(SEQ // TB if ntiles is None else ntiles):
        t0 = tb * TB
        for rc in range(10):
            rows = RC_ROWS[rc]
            self.dma("sp", raw[rc][0:rows, :], self.PR[rc][0:rows, t0:t0 + TB + 2], ("raw", rc),
                     [("PR", rc, b_) for b_ in range(NBLK)] + [("PRz", rc)], (("raw", rc),))
            r_, s_ = raw[rc], sh[rc]
            self.o_ts("dve", s_[0:rows, :], r_[0:rows, 1:TB + 1], c0[0:rows, rc:rc + 1], None, ALU.mult, None,
                      (("raw", rc), "c0"), (("sh", rc),))
            self.o_stt("dve", s_[0:rows, :], r_[0:rows, 0:TB], mu[0:rows, 0, rc:rc + 1], s_[0:rows, :], ALU.mult,
                       ALU.add, (("raw", rc), "mu", ("sh", rc)), (("sh", rc),))
            self.o_stt("dve", s_[0:rows, :], r_[0:rows, 2:TB + 2], mu[0:rows, 1, rc:rc + 1], s_[0:rows, :], ALU.mult,
                       ALU.add, (("raw", rc), "mu", ("sh", rc)), (("sh", rc),))
        self.o_act(hw[0:96, :], sh[6][0:96, :], AF.Tanh, (("sh", 6),), ("hw",))
        self.o_act(sg0[:], sh[8][:], AF.Sigmoid, (("sh", 8),), ("sg0",))
        self.o_act(sg1[:], sh[9][:], AF.Sigmoid, (("sh", 9),), ("sg1",))
        for ch in range(2):
            T_ = sets[ch]
            sigw, aa, kd, bd = [T_["sigw0"], T_["sigw1"]], [T_["aic0"], T_["aic1"]], [T_["kd0"], T_["kd1"]], [T_["bd0"], T_["bd1"]]
            gsb, kk, kk2, rn, kkn, tmpa, ksum, rk, bon = [T_[n] for n in
                                                          ("gsb", "kk", "kk2", "rn", "kkn", "tmpa", "ksum", "rk", "bon")]
            gi, ge, gv, gs = T_["g_incl"], T_["g_excl"], T_["g_inv"], T_["g_sfx"]
            ahat, rhat, bhat, khat, btil, ktil = [T_[n] for n in ("ahat", "rhat", "bhat", "khat", "btil", "ktil")]
            sigT = sigT2[ch]
            cur["xT"], cur["ch"] = xT2[ch], ch
            S.ktag = ch
            seg_marks.append(len(S.ops))
            cs = slice(ch * 128, (ch + 1) * 128)
            r_s, k_s, v_s = sh[ch], sh[2 + ch], sh[4 + ch]
            rK, kK, vK = ("sh", ch), ("sh", 2 + ch), ("sh", 4 + ch)
            for d in range(2):
                b, p = bank()
                self.o_mm(p, self.w2a2[:, d, cs], hw[0:96, :], ("w2a2", "hw"), (("ps", b),))
                self.o_act(sigw[d][:], p, AF.Sigmoid, (("ps", b), "pv"), (("sigw", d),), bias=pv[:, 5 + d, ch:ch + 1])
                b, p = bank()
                self.o_mm(p, self.w2a2[:, 2 + d, cs], sh[7][0:96, :], ("w2a2", ("sh", 7)), (("ps", b),))
                self.o_act(aa[d][:], p, AF.Sigmoid, (("ps", b), "pv"), (("aic", d),), bias=pv[:, 7 + d, ch:ch + 1])
            b, p = bank()
            self.o_mm(p, self.g2[:, 0, cs], sg0[:], ("g2", "sg0"), (("ps", b),), start=True, stop=False)
            self.o_mm(p, self.g2[:, 1, cs], sg1[:], ("g2", "sg1"), (("ps", b),), start=False, stop=True)
            self.evac_copy(gsb[:], p, (("ps", b),), ("gsb",))
            self.dma("act", self.GT[ch][:, t0:t0 + TB], gsb[:], ("gto", ch), ("gsb",), (("GT", ch, tb),))
            self.o_ts("dve", kk[:], k_s[:], pv[:, 0, ch:ch + 1], None, ALU.mult, None, (kK, "pv"), ("kk",))
            self.o_tt("dve", kk2[:], kk[:], kk[:], ALU.mult, ("kk",), ("kk2",))
            b, p = bank()
            self.o_mm(p, self.BO[:], kk2[:], ("BO", "kk2"), (("ps", b),))
            self.o_act(rn[:], p, AF.Sqrt, (("ps", b),), ("rn",))
            self.o_ts("dve", rn[:], rn[:], 1e-12, None, ALU.max, None, ("rn",), ("rn",))
            self.o_rec(rn[:], rn[:], ("rn",), ("rn",))
            self.o_tt("dve", kkn[:], kk[:], rn[:], ALU.mult, ("kk", "rn"), ("kkn",))
            for d in range(2):
                self.o_ts("dve", tmpa[:], aa[d][:], -1.0, pv[:, 1, ch:ch + 1], ALU.add, ALU.mult,
                          (("aic", d), "pv"), ("tmpa",))
                self.o_stt("dve", kd[d][:], tmpa[:], 1.0, k_s[:], ALU.add, ALU.mult, ("tmpa", kK), (("kd", d),))
                self.o_tt("dve", bd[d][:], kkn[:], aa[d][:], ALU.mult, ("kkn", ("aic", d)), (("bd", d),))
            self.o_tt("dve", ksum[:], kd[0][:], kd[1][:], ALU.add, (("kd", 0), ("kd", 1)), ("ksum",))
            self.o_stt("dve", rk[:], r_s[:], pv[:, 2, ch:ch + 1], ksum[:], ALU.mult, ALU.mult, (rK, "pv", "ksum"), ("rk",))
            b, p = bank()
            self.o_mm(p, self.BO[:], rk[:], ("BO", "rk"), (("ps", b),))
            self.o_tt("dve", bon[:], p, v_s[:], ALU.mult, (("ps", b), vK), ("bon",))
            self.dma("act", self.BN[ch][:, t0:t0 + TB], bon[:], ("bno", ch), ("bon",), (("BN", ch, tb),))
            to_tokmajor(v_s, vK, self.VT[ch][t0:t0 + TB, :], ("VT", ch, tb))
            for d in range(2):
                b, p = bank()
                for j in range(4):
                    self.o_tp(p[:, j * 128:(j + 1) * 128], sigw[d][:, j * 128:(j + 1) * 128], (("sigw", d),), (("ps", b),))
                self.evac_copy(sigT[:].rearrange("p j c -> p (j c)"), p, (("ps", b),), ("sigT",))
                tri = ("Li", "Ls", "Us") if d == 0 else ("Ui", "Us", "Ls")
                banks = []
                for k_ in tri:
                    b, p = bank()
                    banks.append((b, p))
                    for j in range(4):
                        self.o_mm(p[:, j * 128:(j + 1) * 128], sigT[:, j, :], self.TRI[k_][:],
                                  ("sigT", ("TRI", k_)), (("ps", b),))
                (bi, pi), (be, pe_), (bs, ps_) = banks
                self.o_act(gi[:], pi, AF.Exp, (("ps", bi),), ("gi",))
                self.o_act(gv[:], pi, AF.Exp, (("ps", bi),), ("gv",), scale=-1.0)
                self.o_act(ge[:], pe_, AF.Exp, (("ps", be),), ("ge",))
                self.o_act(gs[:], ps_, AF.Exp, (("ps", bs),), ("gs",))
                last = CHK - 1 if d == 0 else 0
                gsrc = gi[:].rearrange("p (c t) -> p c t", t=CHK)[:, :, last]
                self.o_ts("dve", self.gc[ch][d][:, tb * 8:(tb + 1) * 8], gsrc, 1.0, None, ALU.mult, None,
                          ("gi",), (("gc", ch, d),))
                self.o_stt("dve", ahat[:], kkn[:], -1.0, ge[:], ALU.mult, ALU.mult, ("kkn", "ge"), ("ahat",))
                self.o_tt("dve", rhat[:], r_s[:], gi[:], ALU.mult, (rK, "gi"), ("rhat",))
                self.o_tt("dve", bhat[:], bd[d][:], gv[:], ALU.mult, (("bd", d), "gv"), ("bhat",))
                self.o_tt("dve", khat[:], kd[d][:], gv[:], ALU.mult, (("kd", d), "gv"), ("khat",))
                self.o_tt("dve", btil[:], bd[d][:], gs[:], ALU.mult, (("bd", d), "gs"), ("btil",))
                self.o_tt("dve", ktil[:], kd[d][:], gs[:], ALU.mult, (("kd", d), "gs"), ("ktil",))
                for xi, (t_, nm) in enumerate(((ahat, "ahat"), (rhat, "rhat"), (bhat, "bhat"), (khat, "khat"))):
                    self.dma("sp", self.FM[ch][d][xi][:, t0:t0 + TB], t_[:], ("fmo", xi, ch), (nm,), (("FM", ch, d, tb),))
                for xi, (t_, nm) in enumerate(((ahat, "ahat"), (btil, "btil"), (ktil, "ktil"))):
                    to_tokmajor(t_, nm, self.TM[ch][d][xi][t0:t0 + TB, :], ("TM", ch, d, tb))
        S.ktag = None
        a_, b_ = seg_marks[-2], seg_marks[-1]
        s0, s1 = S.ops[a_:b_], S.ops[b_:]
        merged = []
        for k_ in range(max(len(s0), len(s1))):
            if k_ < len(s0):
                merged.append(s0[k_])
            if k_ < len(s1):
                merged.append(s1[k_])
        S.ops[a_:] = merged
        if tb % 4 == 3:
            S.emit()
    S.emit()


BuilderB.rwkv_inputs = _rwkv_inputs
BuilderB.rwkv_consts = _rwkv_consts
BuilderB.rwkv_prep = _rwkv_prep


def _rwkv_scan(self, st, nsteps=None):
    S = self.S
    f = lambda n, s: self.sb(n, list(s), F32, st)
    CD = [(ch, d) for ch in range(2) for d in range(2)]
    INST = [(ch, d, hh) for ch in range(2) for d in range(2) for hh in range(2)]
    G = 8
    ar = {cd: f("ar%d%d" % cd, (128, G, 2, CHK)) for cd in CD}
    bkf = {cd: f("bkf%d%d" % cd, (128, G, 2, CHK)) for cd in CD}
    BK = {cd: f("BK%d%d" % cd, (128, G, 128)) for cd in CD}
    AT_ = {cd: f("ATt%d%d" % cd, (64, G, 128)) for cd in CD}
    UV = {cd: f("UV%d%d" % cd, (128, G, 128)) for cd in CD}
    Ysb = {cd: f("Ysb%d%d" % cd, (64, G, 128)) for cd in CD}
    ATm = {i: [self.sb("ATm%d%d%d_%d" % (i + (v,)), [128, 128], BF16, st) for v in range(2)] for i in INST}
    fb = lambda n, s: self.sb(n, list(s), BF16, st)
    Zp = {i: [fb("Zp%d%d%d_%d" % (i + (v,)), (64, 64)) for v in range(2)] for i in INST}
    Ztp = {i: [fb("Ztp%d%d%d_%d" % (i + (v,)), (64, 64)) for v in range(2)] for i in INST}
    NTp = {i: [fb("NTp%d%d%d_%d" % (i + (v,)), (64, 64)) for v in range(2)] for i in INST}
    Z0b = {i: fb("Z0b%d%d%d" % i, (64, 64)) for i in INST}
    Wsb = {i: fb("Wsb%d%d%d" % i, (64, 64)) for i in INST}
    ATb = {cd: fb("ATb%d%d" % cd, (64, G, 128)) for cd in CD}
    arb = {cd: fb("arb%d%d" % cd, (128, G, 2, CHK)) for cd in CD}
    bkfb = {cd: fb("bkfb%d%d" % cd, (128, G, 2, CHK)) for cd in CD}
    BKb = {cd: fb("BKb%d%d" % cd, (128, G, 128)) for cd in CD}
    UVb = {cd: fb("UVb%d%d" % cd, (128, G, 128)) for cd in CD}
    S0b = {cd: fb("S0b%d%d" % cd, (128, 64)) for cd in CD}
    for cd in CD:
        self.o_memset("dve", S0b[cd][:], 0.0, (("S0b", cd, 0), ("S0b", cd, 1)))
    Xsb = {i: [f("Xsb%d%d%d_%d" % (i + (v,)), (64, 64)) for v in range(2)] for i in INST}
    PTs = {i: [self.sb("PTs%d%d%d_%d" % (i + (v,)), [128, 64], BF16, st) for v in range(2)] for i in INST}
    idf = self.ident_f

    def region():
        q = self.rot("psr", 32)
        b, sl_ = q % 8, q // 8
        return ("ps", b), self.ps[b][:, sl_ * 128:(sl_ + 1) * 128]

    nst = NCHK if nsteps is None else nsteps
    for n in range(nst):
        v = n % 2
        grp, cc = n // G, n % G
        CI = {}
        for (ch, d) in CD:
            tb = grp if d == 0 else SEQ // 512 - 1 - grp
            c = cc if d == 0 else G - 1 - cc
            CI[(ch, d)] = (c, tb * G + c)
            if cc != 0:
                continue
            cd = (ch, d)
            t0 = tb * 512
            fmk, tmk = ("FM", ch, d, tb), ("TM", ch, d, tb)
            FMd, TMd = self.FM[ch][d], self.TM[ch][d]
            fmv = lambda x: FMd[x][:, t0:t0 + 512].rearrange("p (c t) -> p c t", t=CHK)
            tmv = lambda x: TMd[x][t0:t0 + 512, :].rearrange("(c s) k -> s c k", s=CHK)
            self.dma("sp", ar[cd][:, :, 0, :], fmv(0), ("l_ar", cd), (fmk,), (("ar", cd),))
            self.dma("sp", ar[cd][:, :, 1, :], fmv(1), ("l_ar", cd), (fmk,), (("ar", cd),))
            self.dma("sp", bkf[cd][:, :, 0, :], fmv(2), ("l_bkf", cd), (fmk,), (("bkf", cd),))
            self.dma("sp", bkf[cd][:, :, 1, :], fmv(3), ("l_bkf", cd), (fmk,), (("bkf", cd),))
            self.dma("sp", AT_[cd][:], tmv(0), ("l_at", cd), (tmk,), (("AT_", cd),))
            self.S.add("act", lambda e, o=ATb[cd][:], i_=AT_[cd][:]: e.copy(out=o, in_=i_),
                       (("AT_", cd),), (("ATb", cd),))
            self.dma("sp", BK[cd][0:64, :, :], tmv(1), ("l_bk", cd), (tmk,), (("BK", cd),))
            self.dma("sp", BK[cd][64:128, :, :], tmv(2), ("l_bk", cd), (tmk,), (("BK", cd),))
            self.dma("sp", UV[cd][64:128, :, :],
                     self.VT[ch][t0:t0 + 512, :].rearrange("(c s) k -> s c k", s=CHK), ("l_uv", cd),
                     (("VT", ch, tb),), (("UVv", cd),))
            self.S.add("act", lambda e, o=arb[cd][:], i_=ar[cd][:]: e.copy(out=o, in_=i_), (("ar", cd),), (("arb", cd),))
            self.S.add("dve", lambda e, o=bkfb[cd][:], i_=bkf[cd][:]: e.tensor_copy(out=o, in_=i_),
                       (("bkf", cd),), (("bkfb", cd),))
            self.S.add("act", lambda e, o=BKb[cd][:], i_=BK[cd][:]: e.copy(out=o, in_=i_), (("BK", cd),), (("BKb", cd),))
            self.S.add("dve", lambda e, o=UVb[cd][64:128, :, :], i_=UV[cd][64:128, :, :]: e.tensor_copy(out=o, in_=i_),
                       (("UVv", cd),), (("UVbv", cd),))
        reg = {}
        for i in INST:
            ch, d, hh = i
            cd, hp = (ch, d), 64 * hh
            k1, p1 = region()
            self.o_mm(p1, bkfb[cd][hp:hp + 64, CI[cd][0], :, :], arb[cd][hp:hp + 64, CI[cd][0], :, :],
                      (("bkfb", cd), ("arb", cd)), (k1,))
            k2, p2 = region()
            self.o_mm(p2[0:64, 0:64], arb[cd][hp:hp + 64, CI[cd][0], 0, :], bkfb[cd][hp:hp + 64, CI[cd][0], 0, :],
                      (("bkfb", cd), ("arb", cd)), (k2,))
            reg[i] = (k1, p1, k2, p2)
        for i in INST:
            ch, d, hh = i
            k1, p1, k2, p2 = reg[i]
            self.o_tt("dve", ATm[i][v][:], p1, self.MT[d][:], ALU.mult, (k1, ("MT", d)), (("ATm", i, v),))
            self.o_tt("dve", Ztp[i][0][:], p2[0:64, 0:64], self.MAB[d][:], ALU.mult, (k2, ("MAB", d)), (("Ztp", i, 0),))
            self.o_tt("pool", NTp[i][0][:], ATm[i][v][0:64, 0:64], idf[0:64, 0:64], ALU.add,
                      (("ATm", i, v), "ident_f"), (("NTp", i, 0),))
            self.o_tt("dve", Z0b[i][:], p1[0:64, 0:64], self.MT[d][0:64, 0:64], ALU.mult, (k1, ("MT", d)),
                      (("Z0b", i),))
        for j in range(1, 6):
            a, b_ = (j - 1) % 2, j % 2
            for i in INST:
                Zc = Z0b[i][:] if j == 1 else Zp[i][a][:]
                Zck = ("Z0b", i) if j == 1 else ("Zp", i, a)
                Ztc, Ztck = Ztp[i][a][:], ("Ztp", i, a)
                if j < 5:
                    k1, p1 = region()
                    self.o_mm(p1[0:64, 0:64], Ztc, Zc, (Zck, Ztck), (k1,))
                else:
                    k1, p1 = None, None
                k2, p2 = region()
                self.o_mm(p2[0:64, 0:64], Zc, Ztc, (Zck, Ztck), (k2,))
                reg[i] = (k1, p1, k2, p2)
            for i in INST:
                k1, p1, k2, p2 = reg[i]
                if k1 is not None:
                    self.S.add("act", lambda e, o=Zp[i][b_][:], s=p1[0:64, 0:64]: e.copy(out=o, in_=s),
                               (k1,), (("Zp", i, b_),))
                self.S.add("dve", lambda e, o=Ztp[i][b_][:], s=p2[0:64, 0:64]: e.tensor_copy(out=o, in_=s),
                           (k2,), (("Ztp", i, b_),))
            for i in INST:
                k3, p3 = region()
                self.o_mm(p3[0:64, 0:64], Ztp[i][b_][:], NTp[i][a][:], (("Ztp", i, b_), ("NTp", i, a)), (k3,))
                reg[i] = (k3, p3)
            for i in INST:
                k3, p3 = reg[i]
                self.o_tt("dve", NTp[i][b_][:], p3[0:64, 0:64], NTp[i][a][:], ALU.add, (k3, ("NTp", i, a)),
                          (("NTp", i, b_),))
        NTf = 1
        for i in INST:
            ch, d, hh = i
            cd, hk = (ch, d), slice(64 * hh, 64 * hh + 64)
            k1, p1 = region()
            self.o_mm(p1[0:64, 0:64], ATm[i][v][64:128, 0:64], UVb[cd][64:128, CI[cd][0], hk],
                      (("ATm", i, v), ("UVbv", cd)), (k1,))
            reg[i] = (k1, p1)
        for i in INST:
            k1, p1 = reg[i]
            self.S.add("act", lambda e, o=Wsb[i][:], s=p1[0:64, 0:64]: e.copy(out=o, in_=s), (k1,), (("Wsb", i),))
        for i in INST:
            ch, d, hh = i
            cd, hp, hk = (ch, d), 64 * hh, slice(64 * hh, 64 * hh + 64)
            k1, p1 = region()
            self.o_mm(p1[0:64, 0:64], NTp[i][NTf][:], Wsb[i][:], (("NTp", i, NTf), ("Wsb", i)), (k1,))
            k2, p2 = region()
            self.o_mm(p2[hp:hp + 64, 0:64], ATb[cd][0:64, CI[cd][0], hk], NTp[i][NTf][:],
                      (("ATb", cd), ("NTp", i, NTf)), (k2,))
            reg[i] = (k1, p1, k2, p2)
        for i in INST:
            ch, d, hh = i
            hp = 64 * hh
            k1, p1, k2, p2 = reg[i]
            self.S.add("act", lambda e, o=Xsb[i][v][:], s=p1[0:64, 0:64]: e.copy(out=o, in_=s), (k1,), (("Xsb", i, v),))
            self.S.add("dve", lambda e, o=PTs[i][v][hp:hp + 64, :], s=p2[hp:hp + 64, 0:64]: e.tensor_copy(out=o, in_=s),
                       (k2,), (("PTs", i, v),))
        for i in INST:
            ch, d, hh = i
            cd, hp = (ch, d), 64 * hh
            k1, p1 = region()
            self.o_mm(p1[0:64, 0:64], PTs[i][v][hp:hp + 64, :], S0b[cd][hp:hp + 64, :],
                      (("PTs", i, v), ("S0b", cd, hh)), (k1,))
            reg[i] = (k1, p1)
        for i in INST:
            ch, d, hh = i
            cd, hk = (ch, d), slice(64 * hh, 64 * hh + 64)
            k1, p1 = reg[i]
            self.o_tt("dve", UVb[cd][0:64, CI[cd][0], hk], p1[0:64, 0:64], Xsb[i][v][:], ALU.add, (k1, ("Xsb", i, v)),
                      (("UVu", cd, hh),))
        for i in INST:
            ch, d, hh = i
            cd, hp, hk = (ch, d), 64 * hh, slice(64 * hh, 64 * hh + 64)
            uvk = (("UVu", cd, hh), ("UVbv", cd))
            k1, p1 = region()
            self.o_mm(p1[0:64, 0:64], arb[cd][hp:hp + 64, CI[cd][0], 1, :], S0b[cd][hp:hp + 64, :],
                      (("arb", cd), ("S0b", cd, hh)), (k1,), start=True, stop=False)
            self.o_mm(p1[0:64, 0:64], ATm[i][v][:, 64:128], UVb[cd][:, CI[cd][0], hk], (("ATm", i, v),) + uvk, (k1,),
                      start=False, stop=True)
            k2, p2 = region()
            self.o_mm(p2[hp:hp + 64, 0:64], BKb[cd][:, CI[cd][0], hk], UVb[cd][:, CI[cd][0], hk], (("BKb", cd),) + uvk, (k2,))
            reg[i] = (k1, p1, k2, p2)
        for i in INST:
            ch, d, hh = i
            cd, hp, hk = (ch, d), 64 * hh, slice(64 * hh, 64 * hh + 64)
            cn = CI[cd][1]
            k1, p1, k2, p2 = reg[i]
            self.S.add("act", lambda e, o=Ysb[cd][:, CI[cd][0], hk], s=p1[0:64, 0:64]: e.copy(out=o, in_=s),
                       (k1,), (("Ysb", cd, hh),))
            st_ = self.S0T[ch][d][hp:hp + 64, :]
            self.o_stt("dve", st_, st_, self.gc[ch][d][hp:hp + 64, cn:cn + 1], p2[hp:hp + 64, 0:64], ALU.mult, ALU.add,
                       (("S0T", ch, d, hh), ("gc", ch, d), k2), (("S0T", ch, d, hh),))
            self.S.add("act", lambda e, o=S0b[cd][hp:hp + 64, :], s_=st_: e.copy(out=o, in_=s_),
                       (("S0T", ch, d, hh),), (("S0b", cd, hh),))
        if cc == G - 1 or n == nst - 1:
            for (ch, d) in CD:
                cd = (ch, d)
                tb = grp if d == 0 else SEQ // 512 - 1 - grp
                t0 = tb * 512
                self.dma("act", self.YO[ch][d][t0:t0 + 512, :].rearrange("(c s) k -> s c k", s=CHK), Ysb[cd][:],
                         ("yo", cd), (("Ysb", cd, 0), ("Ysb", cd, 1)), (("YO", ch, d, tb),))
        if n % 8 == 7:
            S.emit()
    S.emit()


def _rwkv_final(self, st, ntiles=None):
    S = self.S
    TB = 512
    f = lambda n, s=(128, TB): self.sb(n, list(s), F32, st)
    sets = []
    for c in range(2):
        sets.append(dict(yf=f("yf_%d" % c, (128, 4, 128)), ybk=f("ybk_%d" % c, (128, 4, 128)), ysb=f("ysb_%d" % c),
                         dd=f("dd_%d" % c), d2=f("d2_%d" % c), rs=f("rs_%d" % c), bn=f("bn_%d" % c), gt=f("gt_%d" % c),
                         ob=self.sb("rob_%d" % c, [128, TB], BF16, st)))
    pv = self.pv
    nt = SEQ // TB if ntiles is None else ntiles
    work = [(ch, tb) for ch in range(2) for tb in range(nt)]

    def do(ch, tb, c):
        T_ = sets[c]
        yf, ybk, ysb, dd, d2, rs, bn, gt, ob = [T_[k] for k in ("yf", "ybk", "ysb", "dd", "d2", "rs", "bn", "gt", "ob")]
        t0 = tb * TB

        def bank():
            b = 4 * c + self.rot("r3ps%d" % c, 4)
            return b, self.ps[b]
        self.dma("sp", yf[:], self.YO[ch][0][t0:t0 + TB, :].rearrange("(j p) c -> p j c", p=128), ("r3a", c),
                 (("YO", ch, 0, tb),), ("yf",))
        self.dma("sp", ybk[:], self.YO[ch][1][t0:t0 + TB, :].rearrange("(j p) c -> p j c", p=128), ("r3b", c),
                 (("YO", ch, 1, tb),), ("ybk",))
        self.dma("sp", bn[:], self.BN[ch][:, t0:t0 + TB], ("r3c", c), (("BN", ch, tb),), ("bn",))
        self.dma("sp", gt[:], self.GT[ch][:, t0:t0 + TB], ("r3d", c), (("GT", ch, tb),), ("gt",))
        self.o_tt("dve", yf[:], yf[:], ybk[:], ALU.add, ("yf", "ybk"), ("yf",))
        b, p = bank()
        for j in range(4):
            self.o_tp(p[:, j * 128:(j + 1) * 128], yf[:, j, :], ("yf",), (("ps", b),))
        self.evac_copy(ysb[:], p, (("ps", b),), ("ysb",))
        b, p = bank()
        self.o_mm(p, self.BO64[:], ysb[:], ("BO64", "ysb"), (("ps", b),))
        self.o_tt("dve", dd[:], ysb[:], p, ALU.subtract, ("ysb", ("ps", b)), ("dd",))
        self.o_tt("dve", d2[:], dd[:], dd[:], ALU.mult, ("dd",), ("d2",))
        b, p = bank()
        self.o_mm(p, self.BO64[:], d2[:], ("BO64", "d2"), (("ps", b),))
        self.o_act(rs[:], p, AF.Sqrt, (("ps", b),), ("rs",), bias=GN_EPS)
        self.o_rec(rs[:], rs[:], ("rs",), ("rs",))
        self.o_tt("dve", dd[:], dd[:], rs[:], ALU.mult, ("dd", "rs"), ("dd",))
        self.o_ts("dve", dd[:], dd[:], pv[:, 3, ch:ch + 1], pv[:, 4, ch:ch + 1], ALU.mult, ALU.add,
                  ("dd", "pv"), ("dd",))
        self.o_tt("dve", dd[:], dd[:], bn[:], ALU.add, ("dd", "bn"), ("dd",))
        self.o_tt("dve", ob[:], dd[:], gt[:], ALU.mult, ("dd", "gt"), ("rob",))
        self.dma("act", self.yT[ch][:, t0:t0 + TB], ob[:], ("r3o", c), ("rob",), ())

    for w in range(0, len(work), 2):
        a_ = len(S.ops)
        S.ktag = 0
        do(work[w][0], work[w][1], 0)
        b_ = len(S.ops)
        if w + 1 < len(work):
            S.ktag = 1
            do(work[w + 1][0], work[w + 1][1], 1)
        S.ktag = None
        s0, s1 = S.ops[a_:b_], S.ops[b_:]
        merged = []
        for k_ in range(max(len(s0), len(s1))):
            if k_ < len(s0):
                merged.append(s0[k_])
            if k_ < len(s1):
                merged.append(s1[k_])
        S.ops[a_:] = merged
    S.emit()


def _rwkv_phase(self, dbg=None):
    with ExitStack() as stc:
        self.rwkv_consts(stc)
        with ExitStack() as st:
            self.rwkv_prep(st, None if dbg is None else dbg[0])
        with ExitStack() as st:
            self.rwkv_scan(st, None if dbg is None else dbg[1])
        with ExitStack() as st:
            self.rwkv_final(st, None if dbg is None else dbg[2])


BuilderB.rwkv_scan = _rwkv_scan
BuilderB.rwkv_final = _rwkv_final
BuilderB.rwkv_phase = _rwkv_phase


def rwkv_params_for_core(inp, c):
    sl = slice(256 * c, 256 * c + 256)
    R = 2048
    colidx = np.zeros((10, 128), np.int64)
    valid = np.zeros((10, 128), bool)
    for rc in range(6):
        colidx[rc] = (rc // 2) * R + 256 * c + (rc % 2) * 128 + np.arange(128)
        valid[rc] = True
    colidx[6, :96] = 3 * R + np.arange(96); valid[6, :96] = True
    colidx[7, :96] = 3 * R + 96 + np.arange(96); valid[7, :96] = True
    for rc in (8, 9):
        colidx[rc] = 3 * R + 192 + (rc - 8) * 128 + np.arange(128); valid[rc] = True
    mu = np.zeros((128, 2, 10), np.float32)
    for i, nm in enumerate(("mu_prev", "mu_next")):
        m = inp[nm][0][colidx]
        mu[:, i, :] = np.where(valid, m, 0).T
    names = ("k_k", "k_a", "r_k", "gn_w", "gn_b", "w0_f", "w0_b", "a0_f", "a0_b")
    pv = np.zeros((128, 9, 2), np.float32)
    for i, nm in enumerate(names):
        vec = inp[nm][0].reshape(-1)[sl]
        pv[:, i, :] = vec.reshape(2, 128).T
    w2a2 = np.stack([inp[nm][0][:, sl] for nm in ("w2_f", "w2_b", "a2_f", "a2_b")], axis=1)
    g2w = inp["g2"][0][:, sl].reshape(2, 128, 256).transpose(1, 0, 2)
    return {"mu": np.ascontiguousarray(mu), "pv": np.ascontiguousarray(pv),
            "w2a2": np.ascontiguousarray(w2a2.astype(np.float32)), "g2w": np.ascontiguousarray(g2w.astype(np.float32))}


def kernel(**inp):
    cores = list(range(NCORES))
    x = np.ascontiguousarray(inp["x"][0])
    f32 = lambda a: np.ascontiguousarray(a, dtype=np.float32)
    A = build_A()
    shared = {"g_pre": f32(inp["ffn1_pre_g"]), "g_post": f32(inp["ffn1_post_g"]), "g_next": f32(inp["mix_pre_g"]),
              "wg": f32(inp["ffn1_w_gate"][0]), "wu": f32(inp["ffn1_w_up"][0]), "wd": f32(inp["ffn1_w_down"][0])}
    ims = [dict(shared, x=x[c * NT:(c + 1) * NT]) for c in cores]
    rA = run_bass_kernel_spmd(A.nc, ims, core_ids=cores).results
    h1 = [rA[c]["h1"] for c in cores]
    xnT_all = np.ascontiguousarray(np.stack([rA[c]["xn2T"] for c in cores], axis=0))
    del A, ims, rA
    Bp = build_B()
    w_in = inp["w_in"][0]
    ims = []
    for c in cores:
        Wb, _ = wb_for_core(w_in, c)
        im = {"xnT_all": xnT_all, "Wb": Wb,
              "hidx": np.tile(np.array([[2 * c + 1, 2 * c + 2]], np.float32), (128, 1)),
              "lq1": f32(inp["lq1"]), "lk1": f32(inp["lk1"]), "lq2": f32(inp["lq2"]), "lk2": f32(inp["lk2"]),
              "subln_g": f32(inp["subln_g"])}
        im.update(rwkv_params_for_core(inp, c))
        ims.append(im)
    rB = run_bass_kernel_spmd(Bp.nc, ims, core_ids=cores).results
    mixT_full = np.empty((128, KC, SEQ), dtype=rB[0]["yT"].dtype)
    for c in cores:
        yT = rB[c]["yT"]
        mixT_full[:, 2 * c, :] = yT[0]
        mixT_full[:, 2 * c + 1, :] = yT[1]
        ybT = rB[c]["ybT"]
        mixT_full[:, 16 + 2 * c, :] = ybT[0]
        mixT_full[:, 16 + 2 * c + 1, :] = ybT[1]
    del Bp, ims, rB, xnT_all
    C = build_C()
    shared = {"w_out": f32(inp["w_out"][0]), "g_mpost": f32(inp["mix_post_g"]), "g_pre": f32(inp["ffn2_pre_g"]),
              "g_post": f32(inp["ffn2_post_g"]), "g_fin": f32(inp["final_g"]),
              "wg": f32(inp["ffn2_w_gate"][0]), "wu": f32(inp["ffn2_w_up"][0]), "wd": f32(inp["ffn2_w_down"][0])}
    ims = [dict(shared, h1=h1[c], mixT=np.ascontiguousarray(mixT_full[:, :, c * NT:(c + 1) * NT])) for c in cores]
    rC = run_bass_kernel_spmd(C.nc, ims, core_ids=cores).results
    out = np.concatenate([rC[c]["out"] for c in cores], axis=0)
    return out[None].astype(np.float32)
```
